# Optimizing a Trainium2 kernel written in Bass

```python
import math
import jax, jax.numpy as jnp
from jax import lax
import numpy as np

D_MODEL = 2048
BATCH = 4
SEQ = 4096
DEPTH = 2

CHUNK = 64
Q_BLOCK = 128
MLA_HEADS = 8
MLA_Q_LORA = 512
MLA_KV_LORA = 256
MLA_NOPE_DIM = 128
MLA_ROPE_DIM = 64
MLA_V_DIM = 128
RET_HEADS = 4
RET_QK_DIM = 256
RET_V_DIM = 256
MIX_WIDTH = MLA_HEADS * MLA_V_DIM + RET_HEADS * RET_V_DIM
D_FF = -(-(8 * D_MODEL) // (3 * 256)) * 256
ROPE_THETA = 10000.0
LN_EPS = 1e-5
RMS_EPS = 1e-6
GN_EPS = 1e-5
ALPHA = (2 * DEPTH) ** 0.25
BETA = (8 * DEPTH) ** -0.25
IN_SIZES = (MLA_Q_LORA, MLA_KV_LORA, MLA_ROPE_DIM,
            RET_HEADS * RET_QK_DIM, RET_HEADS * RET_QK_DIM,
            RET_HEADS * RET_V_DIM, RET_HEADS * RET_V_DIM)
D_IN = sum(IN_SIZES)

kernel_name = "hybrid_mla_retention_deepnorm"


def layer_norm(x, g, b):
    xf = x.astype(jnp.float32)
    mu = xf.mean(-1, keepdims=True)
    var = jnp.square(xf - mu).mean(-1, keepdims=True)
    return ((xf - mu) * lax.rsqrt(var + LN_EPS) * g + b).astype(x.dtype)


def rms_norm(x, g):
    xf = x.astype(jnp.float32)
    return (xf * lax.rsqrt(jnp.square(xf).mean(-1, keepdims=True) + RMS_EPS) * g).astype(x.dtype)


def rope_tables(positions, dim):
    inv_freq = ROPE_THETA ** (-jnp.arange(0, dim, 2, dtype=jnp.float32) / dim)
    ang = positions.astype(jnp.float32)[..., None] * inv_freq
    return jnp.cos(ang), jnp.sin(ang)


def apply_rope(t, cos, sin):
    tf = t.astype(jnp.float32)
    half = t.shape[-1] // 2
    t1, t2 = tf[..., :half], tf[..., half:]
    c, s = cos[:, :, None, :], sin[:, :, None, :]
    return jnp.concatenate([t1 * c - t2 * s, t2 * c + t1 * s], axis=-1).astype(t.dtype)


def split_columns(h):
    parts, start = [], 0
    for size in IN_SIZES:
        parts.append(h[..., start:start + size])
        start += size
    return parts


def mla_group(c_q, c_kv, k_rope, cos, sin, q_norm_g, kv_norm_g, w_uq, w_ukv):
    B, S, _ = c_q.shape
    H = MLA_HEADS
    q = (rms_norm(c_q, q_norm_g) @ w_uq).reshape(B, S, H, MLA_NOPE_DIM + MLA_ROPE_DIM)
    q_nope = q[..., :MLA_NOPE_DIM]
    q_rope = apply_rope(q[..., MLA_NOPE_DIM:], cos, sin)
    kv = (rms_norm(c_kv, kv_norm_g) @ w_ukv).reshape(B, S, H, MLA_NOPE_DIM + MLA_V_DIM)
    k_nope, v = kv[..., :MLA_NOPE_DIM], kv[..., MLA_NOPE_DIM:]
    k_r = apply_rope(k_rope[:, :, None, :], cos, sin)[:, :, 0, :]
    scale = (MLA_NOPE_DIM + MLA_ROPE_DIM) ** -0.5
    chunk_id = jnp.arange(S) // CHUNK
    neg = jnp.finfo(jnp.float32).min
    outs = []
    for blk in range(S // Q_BLOCK):
        q0 = blk * Q_BLOCK
        kend = q0 + Q_BLOCK
        s = (jnp.einsum('bqhd,bkhd->bhqk', q_nope[:, q0:kend], k_nope[:, :kend])
             + jnp.einsum('bqhr,bkr->bhqk', q_rope[:, q0:kend], k_r[:, :kend]))
        s = s.astype(jnp.float32) * scale
        mask = chunk_id[q0:kend, None] >= chunk_id[None, :kend]
        s = jnp.where(mask[None, None], s, neg)
        p = jax.nn.softmax(s, axis=-1).astype(v.dtype)
        outs.append(jnp.einsum('bhqk,bkhd->bqhd', p, v[:, :kend]))
    o = jnp.concatenate(outs, axis=1)
    return o.reshape(B, S, H * MLA_V_DIM)


def retention_group(rq, rk, rv, rg, cos, sin, gn_g, gn_b):
    B, S, _ = rq.shape
    H, DK, DV, L = RET_HEADS, RET_QK_DIM, RET_V_DIM, CHUNK
    NC = S // L
    f32 = jnp.float32
    q = apply_rope(rq.reshape(B, S, H, DK), cos, sin).astype(f32) * (DK ** -0.5)
    k = apply_rope(rk.reshape(B, S, H, DK), cos, sin).astype(f32)
    v = rv.reshape(B, S, H, DV).astype(f32)
    q = q.reshape(B, NC, L, H, DK)
    k = k.reshape(B, NC, L, H, DK)
    v = v.reshape(B, NC, L, H, DV)
    log_gamma = jnp.log1p(-jnp.exp2(-5.0 - jnp.arange(H, dtype=f32)))
    idx = jnp.arange(L, dtype=f32)
    intra_decay = jnp.exp(log_gamma[:, None, None] * jnp.abs(idx[:, None] - idx[None, :]))
    scores = jnp.einsum('bcnhd,bcmhd->bchnm', q, k) * intra_decay[None, None]
    o_intra = jnp.einsum('bchnm,bcmhe->bcnhe', scores, v)
    q_decay = jnp.exp(log_gamma[:, None] * (idx + 1.0))[None]
    k_decay = jnp.exp(log_gamma[:, None] * (L - 1.0 - idx))
    chunk_decay = jnp.exp(log_gamma * L)
    q_decay = q_decay[0]

    def step(state, inp):
        qc, kc, vc = inp
        o_inter = jnp.einsum('bnhd,hn,bhde->bnhe', qc, q_decay, state)
        state = (state * chunk_decay[None, :, None, None]
                 + jnp.einsum('bmhd,hm,bmhe->bhde', kc, k_decay, vc))
        return state, o_inter

    state0 = jnp.zeros((B, H, DK, DV), f32)
    xs = (q.transpose(1, 0, 2, 3, 4), k.transpose(1, 0, 2, 3, 4), v.transpose(1, 0, 2, 3, 4))
    _, o_inter = lax.scan(step, state0, xs)
    o = (o_intra + o_inter.transpose(1, 0, 2, 3, 4)).reshape(B, S, H, DV)
    mu = o.mean(-1, keepdims=True)
    var = jnp.square(o - mu).mean(-1, keepdims=True)
    o = ((o - mu) * lax.rsqrt(var + GN_EPS)).reshape(B, S, H * DV) * gn_g + gn_b
    o = jax.nn.silu(rg.astype(f32)) * o
    return o.astype(rq.dtype)


def setup_inputs(seed: int = 0) -> dict:
    key = jax.random.key(seed)
    ks = list(jax.random.split(key, 24))
    f32 = jnp.float32

    def nrm(k, shape, scale):
        return jax.random.normal(k, shape, f32) * scale

    x = jax.random.normal(ks[0], (BATCH, SEQ, D_MODEL), f32)
    start = jax.random.randint(ks[1], (BATCH, 1), 0, 4096, dtype=jnp.int32)
    positions = (start + jnp.arange(SEQ, dtype=jnp.int32)[None, :]).astype(jnp.int32)
    return {
        "x": x,
        "positions": positions,
        "ln_in_g": 1.0 + nrm(ks[2], (D_MODEL,), 0.02),
        "ln_in_b": nrm(ks[3], (D_MODEL,), 0.02),
        "w_in": nrm(ks[4], (DEPTH, D_MODEL, D_IN), D_MODEL ** -0.5),
        "q_norm_g": 1.0 + nrm(ks[5], (DEPTH, MLA_Q_LORA), 0.02),
        "kv_norm_g": 1.0 + nrm(ks[6], (DEPTH, MLA_KV_LORA), 0.02),
        "w_uq": nrm(ks[7], (DEPTH, MLA_Q_LORA, MLA_HEADS * (MLA_NOPE_DIM + MLA_ROPE_DIM)), MLA_Q_LORA ** -0.5),
        "w_ukv": nrm(ks[8], (DEPTH, MLA_KV_LORA, MLA_HEADS * (MLA_NOPE_DIM + MLA_V_DIM)), MLA_KV_LORA ** -0.5),
        "ret_gn_g": 1.0 + nrm(ks[9], (DEPTH, RET_HEADS * RET_V_DIM), 0.02),
        "ret_gn_b": nrm(ks[10], (DEPTH, RET_HEADS * RET_V_DIM), 0.02),
        "w_out": nrm(ks[11], (DEPTH, MIX_WIDTH, D_MODEL), (MIX_WIDTH ** -0.5) * BETA),
        "ln1_g": 1.0 + nrm(ks[12], (DEPTH, D_MODEL), 0.02),
        "ln1_b": nrm(ks[13], (DEPTH, D_MODEL), 0.02),
        "w_gate": nrm(ks[14], (DEPTH, D_MODEL, D_FF), D_MODEL ** -0.5),
        "w_up": nrm(ks[15], (DEPTH, D_MODEL, D_FF), D_MODEL ** -0.5),
        "w_down": nrm(ks[16], (DEPTH, D_FF, D_MODEL), (D_FF ** -0.5) * BETA),
        "ln2_g": 1.0 + nrm(ks[17], (DEPTH, D_MODEL), 0.02),
        "ln2_b": nrm(ks[18], (DEPTH, D_MODEL), 0.02),
    }


def reference(x, positions, ln_in_g, ln_in_b, w_in, q_norm_g, kv_norm_g, w_uq, w_ukv,
              ret_gn_g, ret_gn_b, w_out, ln1_g, ln1_b, w_gate, w_up, w_down, ln2_g, ln2_b):
    cos_m, sin_m = rope_tables(positions, MLA_ROPE_DIM)
    cos_r, sin_r = rope_tables(positions, RET_QK_DIM)
    x = layer_norm(x, ln_in_g, ln_in_b)
    for l in range(DEPTH):
        h = x @ w_in[l]
        c_q, c_kv, k_rope, rq, rk, rv, rg = split_columns(h)
        a = mla_group(c_q, c_kv, k_rope, cos_m, sin_m, q_norm_g[l], kv_norm_g[l], w_uq[l], w_ukv[l])
        r = retention_group(rq, rk, rv, rg, cos_r, sin_r, ret_gn_g[l], ret_gn_b[l])
        mix = jnp.concatenate([a, r], axis=-1) @ w_out[l]
        x = layer_norm(ALPHA * x + mix, ln1_g[l], ln1_b[l])
        f = (jax.nn.silu(x @ w_gate[l]) * (x @ w_up[l])) @ w_down[l]
        x = layer_norm(ALPHA * x + f, ln2_g[l], ln2_b[l])
    return x
```

```python
import math
from contextlib import ExitStack

import numpy as np
import concourse.bass as bass
import concourse.mybir as mybir
from concourse.bass_utils import run_bass_kernel_spmd

F32 = mybir.dt.float32
BF16 = mybir.dt.bfloat16
I32 = mybir.dt.int32
AF = mybir.ActivationFunctionType
ALU = mybir.AluOpType

D = 2048
KT = 16
DIN = 4928
DFF = 5632
DEPTH = 2
HM = 8
HR = 4
LN_EPS = 1e-5
RMS_EPS = 1e-6
GN_EPS = 1e-5
ALPHA = (2 * DEPTH) ** 0.25
MLA_SCALE = 192 ** -0.5
RET_SCALE = 256 ** -0.5
TWO_PI = 2.0 * math.pi
CW1 = 6.28125
CW2 = TWO_PI - CW1
ENGS = ("pe", "act", "dve", "pool", "sp")
ESZ = {F32: 4, BF16: 2, I32: 4}


def esize(dt):
    return ESZ[dt]


class Prog:
    CELL = 64
    MAXSRC = 48

    def __init__(self, nc, stack):
        self.nc = nc
        self.stack = stack
        self.src_idx = {}
        self.names = []
        self.sems = {}
        self.cnt = {}
        self.ops = {e: [] for e in ENGS}
        self.seen = {e: {} for e in ENGS}
        self.trk = {}
        self.cache = {}
        self.label = ""
        self.waitlab = {}
        for e in ("pe", "act", "dve", "pool"):
            self.add_src(e)

    def add_src(self, name):
        if name in self.src_idx:
            return name
        self.src_idx[name] = len(self.names)
        self.names.append(name)
        assert len(self.names) <= self.MAXSRC
        self.sems[name] = self.stack.enter_context(self.nc.semaphore("s_" + name))
        self.cnt[name] = 0
        return name

    def _reg(self, name, nbytes):
        n = (nbytes + self.CELL - 1) // self.CELL
        self.trk[name] = dict(ws=np.full(n, -1, np.int64), wv=np.zeros(n, np.int64),
                              rv=np.zeros((self.MAXSRC, n), np.int64))

    def sb(self, name, shape, dt):
        t = self.stack.enter_context(self.nc.sbuf_tensor(name, list(shape), dt))
        self._reg(name, int(np.prod(shape[1:])) * esize(dt))
        return t

    def psum(self, name, shape, dt):
        t = self.stack.enter_context(self.nc.psum_tensor(name, list(shape), dt))
        self._reg(name, int(np.prod(shape[1:])) * esize(dt))
        return t

    def cells(self, ap):
        name = ap.tensor.name
        T = self.trk.get(name)
        if T is None:
            return None
        pat = ap.ap
        key = (name, int(ap.offset), pat, str(ap.dtype))
        c = self.cache.get(key)
        if c is not None:
            return T, c
        es = esize(ap.dtype)
        pstep = pat[0][0]
        off = int(ap.offset) % pstep if pstep > 0 else int(ap.offset)
        starts = np.array([off], np.int64)
        free = [p for p in pat[1:]]
        if free:
            for st, cn in free[:-1]:
                starts = (starts[:, None] + st * np.arange(cn, dtype=np.int64)[None, :]).ravel()
            lst, lcn = free[-1]
            run = (lcn - 1) * abs(lst) + 1
        else:
            run = 1
        lo = starts * es // self.CELL
        hi = ((starts + run) * es - 1) // self.CELL
        if len(lo) == 1:
            idx = np.arange(lo[0], hi[0] + 1)
        else:
            idx = np.unique(np.concatenate([np.arange(a, b + 1) for a, b in zip(lo, hi)]))
        self.cache[key] = idx
        return T, idx

    def op(self, eng, fn, reads=(), writes=(), dma=None, inc=None, extra=()):
        src = dma if dma else eng
        deps = {}
        for s_, v_ in extra:
            if v_ > 0:
                deps[s_] = max(deps.get(s_, 0), v_)

        def need(s, v):
            if v > deps.get(s, 0):
                deps[s] = v

        rc = [self.cells(a) for a in reads]
        wc = [self.cells(a) for a in writes]
        for c in rc:
            if c is None:
                continue
            T, idx = c
            ws = T["ws"][idx]
            wv = T["wv"][idx]
            for s in np.unique(ws):
                if s < 0:
                    continue
                nm = self.names[s]
                if nm == "pe" and eng == "pe" and not dma:
                    continue
                need(nm, int(wv[ws == s].max()))
        for c in wc:
            if c is None:
                continue
            T, idx = c
            ws = T["ws"][idx]
            wv = T["wv"][idx]
            for s in np.unique(ws):
                if s < 0:
                    continue
                nm = self.names[s]
                if nm == src and not dma:
                    continue
                need(nm, int(wv[ws == s].max()))
            rv = T["rv"][:, idx].max(axis=1)
            for s in np.nonzero(rv)[0]:
                nm = self.names[s]
                if nm == src and not dma:
                    continue
                need(nm, int(rv[s]))
        if dma:
            if self.cnt[src] > 0:
                need(src, self.cnt[src])
        if inc is None:
            inc = 16 if dma else 1
        self.cnt[src] += inc
        val = self.cnt[src]
        si = self.src_idx[src]
        for c in rc:
            if c is not None:
                T, idx = c
                T["rv"][si, idx] = val
        for c in wc:
            if c is not None:
                T, idx = c
                T["ws"][idx] = si
                T["wv"][idx] = val
                T["rv"][:, idx] = 0
        seen = self.seen[eng]
        waits = []
        for s, v in deps.items():
            if v > seen.get(s, 0):
                waits.append((s, v))
                seen[s] = v
                if eng == "pe":
                    self.waitlab[f"{self.src_idx[s]}:{v}"] = self.label
        self.ops[eng].append((waits, fn, src, inc))

    def emit(self, final_waits):
        nc = self.nc
        block = self.stack.enter_context(nc.Block())

        def replay(name):
            def run(e):
                for waits, fn, src, inc in self.ops[name]:
                    for s, v in waits:
                        e.wait_ge(self.sems[s], v)
                    ins = fn(e)
                    ins.then_inc(self.sems[src], inc)
                if name == "sp":
                    for s in final_waits:
                        e.wait_ge(self.sems[s], self.cnt[s])
            return run

        block.tensor(replay("pe"))
        block.scalar(replay("act"))
        block.vector(replay("dve"))
        block.gpsimd(replay("pool"))
        block.sync(replay("sp"))


def _fm(v):
    return np.ascontiguousarray(np.asarray(v, np.float32).reshape(-1, 128).T)


VEC_LAYOUT = {}


def _vec_layout():
    if VEC_LAYOUT:
        return VEC_LAYOUT
    off = 0

    def add(name, n):
        nonlocal off
        VEC_LAYOUT[name] = (off, n)
        off += n

    add("ln_in_g", 16)
    add("ln_in_b", 16)
    for l in range(DEPTH):
        for nm in ("ln1_g", "ln1_b", "ln2_g", "ln2_b"):
            add(f"{nm}{l}", 16)
        add(f"qg{l}", 4)
        add(f"kvg{l}", 2)
        add(f"gng{l}", 8)
        add(f"gnb{l}", 8)
    VEC_LAYOUT["_n"] = (off, 0)
    return VEC_LAYOUT


def pack_vecs(inp):
    L = _vec_layout()
    out = np.zeros((128, L["_n"][0]), np.float32)

    def put(name, v):
        o, n = L[name]
        out[:, o:o + n] = _fm(v)

    put("ln_in_g", inp["ln_in_g"])
    put("ln_in_b", inp["ln_in_b"])
    for l in range(DEPTH):
        put(f"ln1_g{l}", inp["ln1_g"][l])
        put(f"ln1_b{l}", inp["ln1_b"][l])
        put(f"ln2_g{l}", inp["ln2_g"][l])
        put(f"ln2_b{l}", inp["ln2_b"][l])
        put(f"qg{l}", inp["q_norm_g"][l])
        put(f"kvg{l}", inp["kv_norm_g"][l])
        put(f"gng{l}", inp["ret_gn_g"][l])
        put(f"gnb{l}", inp["ret_gn_b"][l])
    return out


CST = dict(ident=(0, 128), ones=(128, 128), mask=(256, 128), dt=(384, 512), qdec=(896, 512),
           kdec=(1408, 4), invf_r=(1412, 1), invf_m=(1413, 1), sgn_m=(1414, 1), _n=(1415, 0))


def pack_cst():
    c = np.zeros((128, CST["_n"][0]), np.float64)
    idx = np.arange(128)
    c[:, 0:128] = np.eye(128)
    c[:, 128:256] = 1.0
    ch = idx // 64
    c[:, 256:384] = (ch[:, None] <= ch[None, :]).astype(np.float64)
    for h in range(HR):
        lg = math.log1p(-2.0 ** (-5.0 - h))
        dist = np.abs(idx[:, None] - idx[None, :])
        c[:, 384 + 128 * h:384 + 128 * (h + 1)] = np.exp(lg * dist) * (ch[:, None] <= ch[None, :])
        c[:, 896 + 128 * h:896 + 128 * (h + 1)] = np.exp(lg * (idx + 1.0))[None, :]
        c[:, 1408 + h] = np.exp(lg * (127.0 - idx))
    c[:, 1412] = 10000.0 ** (-np.arange(0, 256, 2, dtype=np.float64) / 256.0)
    invm = (10000.0 ** (-np.arange(0, 64, 2, dtype=np.float32) / np.float32(64))).astype(np.float64)
    c[:, 1413] = np.tile(invm, 4)
    c[:, 1414] = np.tile(np.concatenate([-np.ones(32), np.ones(32)]), 2)
    c = c.astype(np.float32)
    c[:, 1412] = (np.float32(10000.0) ** (-np.arange(0, 256, 2, dtype=np.float32) / np.float32(256))).astype(np.float32)
    return c


def build(SEG, NPRE, NOWN, n_layers=DEPTH, n_pairs=4):
    nc = bass.Bass("TRN2", target_bir_lowering=False)
    NSEG = NPRE + NOWN
    KEYS = NSEG * SEG
    NT = SEG // 128
    NKT = KEYS // 128
    VL = _vec_layout()

    def dram(name, shape, dt, kind="ExternalInput"):
        return nc.dram_tensor(name, list(shape), dt, kind=kind).ap()

    xin = dram("xin", [NSEG * SEG, D], F32)
    pos_d = dram("pos", [NSEG * SEG], I32)
    flag_d = dram("flag", [128, 1], F32)
    vecs_d = dram("vecs", [128, VL["_n"][0]], F32)
    cst_d = dram("cst", [128, CST["_n"][0]], F32)
    w_in = dram("w_in", [DEPTH, D, DIN], F32)
    w_uq = dram("w_uq", [DEPTH, 512, 1536], F32)
    w_ukv = dram("w_ukv", [DEPTH, 256, 2048], F32)
    w_out = dram("w_out", [DEPTH, D, D], F32)
    w_gate = dram("w_gate", [DEPTH, D, DFF], F32)
    w_up = dram("w_up", [DEPTH, D, DFF], F32)
    w_down = dram("w_down", [DEPTH, DFF, D], F32)
    out_d = dram("out", [NOWN * SEG, D], F32, kind="ExternalOutput")
    spf = dram("spf", [NOWN, 128, KT * SEG], F32, kind="Internal")
    spb = [dram(f"spb{i}", [128, KT * SEG], BF16, kind="Internal") for i in range(NOWN)]
    gath = [dram(f"gath{i}", [2 * 128, KT * SEG], BF16, kind="Internal") for i in range(NOWN)]

    stack = ExitStack()
    P = Prog(nc, stack)

    xres = P.sb("xres", [128, KT, SEG], F32)
    xT = P.sb("xT", [128, KT, SEG], BF16)
    mixT = P.sb("mixT", [128, KT, SEG], BF16)
    ckvT = P.sb("ckvT", [128, 2, KEYS], BF16)
    kropeT = P.sb("kropeT", [128, KEYS], BF16)
    state = P.sb("state", [128, HR, 2, 256], F32)
    NSLAB = 2
    slabA = P.sb("slabA", [128, NSLAB * KT * 512], BF16)
    slabs = [slabA[:, i * KT * 512:(i + 1) * KT * 512].rearrange("p (k n) -> p k n", k=KT) for i in range(NSLAB)]
    NSLAB3 = 3
    slabs3 = [slabA[:, i * 12 * 384:(i + 1) * 12 * 384].rearrange("p (k n) -> p k n", k=12) for i in range(NSLAB3)]
    for i in range(3):
        P.add_src(f"w{i}_0")
        P.add_src(f"w{i}_1")
    cstf = P.sb("cstf", [128, CST["_n"][0]], F32)
    vecs = P.sb("vecs_sb", [128, VL["_n"][0]], F32)
    flag = P.sb("flag_sb", [128, 1], F32)
    identb = P.sb("identb", [128, 128], BF16)
    onesb = P.sb("onesb", [128, 128], BF16)
    flagb = P.sb("flagb", [128, 128], BF16)
    maskb = P.sb("maskb", [128, 128], BF16)
    tabs = P.sb("tabs", [128, 4, SEG], F32)
    posi = P.sb("posi", [128, SEG], I32)
    SCR = 48 * 1024
    scr = P.sb("scr", [128, SCR // 2], BF16)
    for s in ("c0", "xi0", "xi1", "xo0", "xo1", "spl", "spl2", "wq", "wkv0", "wkv1", "pp", "cc"):
        P.add_src(s)

    ps = P.psum("ps", [128, 8, 512], F32)

    ident = cstf[:, 0:128]
    ones = cstf[:, 128:256]

    def cst(name, a=0, b=None):
        o, n = CST[name]
        return cstf[:, o + a:o + (n if b is None else b)]

    def vec(name, k):
        o, n = VL[name]
        return vecs[:, o + k:o + k + 1]

    def sview(off, shape, dt):
        n = int(np.prod(shape))
        es = esize(dt)
        assert off % 4 == 0 and off + n * es <= SCR, (off, shape)
        v = scr[:, off // 2: off // 2 + n * es // 2]
        if dt != BF16:
            v = v.bitcast(dt)
        if len(shape) == 2:
            return v.rearrange("p (a b) -> p a b", a=shape[0])
        if len(shape) == 3:
            return v.rearrange("p (a b c) -> p a b c", a=shape[0], b=shape[1])
        return v

    rot = [0]

    def bank():
        b = rot[0]
        rot[0] = (b + 1) % 6
        return b

    def MM(out, pairs, eng="pe"):
        reads = []
        for l, r in pairs:
            reads += [l, r]
        n = len(pairs)

        def fn(e):
            ins = None
            for i, (l, r) in enumerate(pairs):
                ins = e.matmul(out, lhsT=l, rhs=r, start=(i == 0), stop=(i == n - 1))
            return ins
        P.op("pe", fn, reads, [out])

    def MMS(groups):
        reads, writes = [], []
        for o, pairs in groups:
            writes.append(o)
            for l, r in pairs:
                reads += [l, r]

        def fn(e):
            ins = None
            for o, pairs in groups:
                n = len(pairs)
                for i, (l, r) in enumerate(pairs):
                    ins = e.matmul(o, lhsT=l, rhs=r, start=(i == 0), stop=(i == n - 1))
            return ins
        P.op("pe", fn, reads, writes)

    def TR(pairs, idn):
        reads = [idn] + [i for _, i in pairs]
        writes = [o for o, _ in pairs]

        def fn(e):
            ins = None
            for o, i in pairs:
                ins = e.transpose(o, i, idn)
            return ins
        P.op("pe", fn, reads, writes)

    def ACT(out, in_, func, bias=None, scale=None, eng="act"):
        reads = [in_]
        kw = {}
        if bias is not None:
            kw["bias"] = bias
            if not isinstance(bias, float):
                reads.append(bias)
        if scale is not None:
            kw["scale"] = scale
            if not isinstance(scale, float):
                reads.append(scale)
        P.op("act", lambda e: e.activation(out=out, in_=in_, func=func, **kw), reads, [out])

    def TT(eng, out, in0, in1, op):
        f = lambda e: e.tensor_tensor(out=out, in0=in0, in1=in1, op=op)
        P.op(eng, f, [in0, in1], [out])

    def TS(eng, out, in0, s1, s2, op0, op1=None):
        reads = [in0] + [s for s in (s1, s2) if s is not None and not isinstance(s, float)]
        if op1 is None:
            f = lambda e: e.tensor_scalar(out=out, in0=in0, scalar1=s1, scalar2=None, op0=op0)
        else:
            f = lambda e: e.tensor_scalar(out=out, in0=in0, scalar1=s1, scalar2=s2, op0=op0, op1=op1)
        P.op(eng, f, reads, [out])

    def STT(eng, out, in0, scalar, in1, op0, op1):
        reads = [in0, in1] + ([] if isinstance(scalar, float) else [scalar])
        f = lambda e: e.scalar_tensor_tensor(out=out, in0=in0, scalar=scalar, in1=in1, op0=op0, op1=op1)
        P.op(eng, f, reads, [out])

    def CP(eng, out, in_):
        if eng == "act":
            ACT(out, in_, AF.Copy)
        else:
            P.op(eng, lambda e: e.tensor_copy(out=out, in_=in_), [in_], [out])

    def MSET(eng, out, val):
        P.op(eng, lambda e: e.memset(out, val), [], [out])

    def RECIP(out, in_):
        P.op("dve", lambda e: e.reciprocal(out=out, in_=in_), [in_], [out])

    def DMA(q, out, in_, sem, extra=()):
        P.op(q, lambda e: e.dma_start(out=out, in_=in_), [in_], [out], dma=sem, extra=extra)

    DMA("sp", cstf[:], cst_d, "c0")
    DMA("sp", vecs[:], vecs_d, "c0")
    DMA("sp", flag[:], flag_d, "c0")
    CP("dve", identb[:], ident)
    CP("dve", onesb[:], ones)
    CP("dve", maskb[:], cst("mask"))
    TS("dve", flagb[:], ones, flag[:, 0:1], None, ALU.mult)

    slab_i = [0]

    slab3_i = [0]

    def load_slab(runs, kt_n, row0, wsrc, narrow=False):
        if narrow:
            i = slab3_i[0] % NSLAB3
            slab3_i[0] += 1
            sl = slabs3[i]
        else:
            i = slab_i[0] % NSLAB
            slab_i[0] += 1
            sl = slabs[i]
        c = 0
        for r_, (c0, n) in enumerate(runs):
            src = wsrc[row0:row0 + kt_n * 128, c0:c0 + n].rearrange("(kt p) m -> p kt m", p=128)
            DMA("pool", sl[:, 0:kt_n, c:c + n], src, f"w{i}_{r_ % 2}")
            c += n
        return sl

    accx = sview(10240, [SEG], F32)
    accq = sview(12288, [SEG], F32)

    def stats_tile(mt):
        sqb = sview(14336 + 0, [SEG], F32) if False else sview(8192, [SEG], F32)
        if mt == 0:
            CP("dve", accx, xres[:, mt, :])
            ACT(accq, xres[:, mt, :], AF.Square)
        else:
            TT("dve", accx, accx, xres[:, mt, :], ALU.add)
            ACT(sqb, xres[:, mt, :], AF.Square)
            TT("dve", accq, accq, sqb, ALU.add)

    def layer_norm(gname, bname, write_xT=True, fused_stats=False, after_kt=None):
        P.label = "ln"
        sq = [sview(0, [SEG], F32), sview(2048, [SEG], F32)]
        mean = sview(4096, [SEG], F32)
        rstd = sview(6144, [SEG], F32)
        tmp = sview(8192, [SEG], F32)
        pm, pq = ps[:, 6, 0:SEG], ps[:, 7, 0:SEG]
        if fused_stats == "psum":
            pass
        elif fused_stats:
            MM(pm, [(ones, accx)])
            MM(pq, [(ones, accq)])
        else:
            MM(pm, [(ones, xres[:, kt, :]) for kt in range(KT)])
            for kt in range(KT):
                ACT(sq[kt % 2], xres[:, kt, :], AF.Square)
                P.op("pe", (lambda e, kt=kt: e.matmul(pq, lhsT=ones, rhs=sq[kt % 2], start=(kt == 0), stop=(kt == KT - 1))),
                     [ones, sq[kt % 2]], [pq])
        TS("dve", mean, pm, 1.0 / D, None, ALU.mult)
        TT("dve", tmp, mean, mean, ALU.mult)
        STT("dve", tmp, pq, 1.0 / D, tmp, ALU.mult, ALU.subtract)
        ACT(tmp, tmp, AF.Sqrt, bias=LN_EPS, scale=1.0)
        RECIP(rstd, tmp)
        for kt in range(KT):
            TT("dve", xres[:, kt, :], xres[:, kt, :], mean, ALU.subtract)
            TT("dve", xres[:, kt, :], xres[:, kt, :], rstd, ALU.mult)
            if write_xT:
                ACT(xT[:, kt, :], xres[:, kt, :], AF.Identity, bias=vec(bname, kt), scale=vec(gname, kt))
        for kt in range(KT):
            ACT(xres[:, kt, :], xres[:, kt, :], AF.Identity, bias=vec(bname, kt), scale=vec(gname, kt))
            if after_kt is not None:
                after_kt(kt)

    def rope_tables(seg):
        DMA("sp", posi[:], pos_d[seg * SEG:(seg + 1) * SEG].partition_broadcast(128), "pp")
        posf = sview(0, [SEG], F32)
        ang = sview(2048, [SEG], F32)
        ki = sview(4096, [SEG], I32)
        kf = sview(6144, [SEG], F32)
        CP("dve", posf, posi[:])
        for which, invf, sgn in ((0, cst("invf_r"), None), (1, cst("invf_m"), cst("sgn_m"))):
            rc = tabs[:, 2 * which, :]
            r = tabs[:, 2 * which + 1, :]
            m = kf
            TS("dve", ang, posf, invf, None, ALU.mult)
            TS("dve", ki, ang, 1.0 / TWO_PI, None, ALU.mult)
            CP("dve", kf, ki)
            STT("dve", r, kf, -CW1, ang, ALU.mult, ALU.add)
            STT("dve", r, kf, -CW2, r, ALU.mult, ALU.add)
            TS("dve", r, r, -math.pi, math.pi, ALU.max, ALU.min)
            TS("dve", m, r, math.pi / 2, -TWO_PI, ALU.is_gt, ALU.mult)
            STT("dve", rc, r, math.pi / 2, m, ALU.add, ALU.add)
            TS("dve", rc, rc, -math.pi, math.pi, ALU.max, ALU.min)
            ACT(rc, rc, AF.Sin)
            if sgn is None:
                ACT(r, r, AF.Sin)
            else:
                ACT(r, r, AF.Sin, scale=sgn)

    xstg = [sview(16384 + 8192 * t, [D], F32) for t in range(4)]
    prefetched = set()

    def prefetch_x(seg):
        if seg >= NSEG or NT > 4:
            return
        for t in range(NT):
            DMA("sp", xstg[t], xin[seg * SEG + t * 128: seg * SEG + (t + 1) * 128, :], f"xi{t % 2}")
        prefetched.add(seg)

    def load_x(seg):
        P.label = "load_x"
        if seg not in prefetched:
            prefetch_x(seg)
        pm, pq = ps[:, 6, 0:SEG], ps[:, 7, 0:SEG]
        sqs = [sview(0, [SEG], F32), sview(2048, [SEG], F32)]

        def stats_g(g):
            for kt in range(4 * g, 4 * g + 4):
                sqb = sqs[kt % 2]
                ACT(sqb, xres[:, kt, :], AF.Square)

                def fn(e, kt=kt, sqb=sqb):
                    e.matmul(pm, lhsT=ones, rhs=xres[:, kt, :], start=(kt == 0), stop=(kt == KT - 1))
                    return e.matmul(pq, lhsT=ones, rhs=sqb, start=(kt == 0), stop=(kt == KT - 1))
                P.op("pe", fn, [ones, xres[:, kt, :], sqb], [pm, pq])

        for g in range(4):
            for t in range(NT):
                st = xstg[t]
                b = bank()
                TR([(ps[:, b, j * 128:(j + 1) * 128], st[:, (4 * g + j) * 128:(4 * g + j + 1) * 128]) for j in range(4)], ident)
                src = ps[:, b, :].rearrange("p (j n) -> p j n", j=4)
                CP("dve" if t % 2 == 0 else "act", xres[:, 4 * g:4 * g + 4, t * 128:(t + 1) * 128], src)
            if g >= 1:
                stats_g(g - 1)
        stats_g(3)
        layer_norm("ln_in_g", "ln_in_b", fused_stats="psum")

    def store_out(own):
        P.label = "store"
        stg = [sview(16384, [D], F32), sview(24576, [D], F32)]
        for t in range(NT):
            st = stg[t % 2]
            for g in range(4):
                b = bank()
                TR([(ps[:, b, j * 128:(j + 1) * 128], xres[:, 4 * g + j, t * 128:(t + 1) * 128]) for j in range(4)], ident)
                CP("dve" if g % 2 == 0 else "act", st[:, g * 512:(g + 1) * 512], ps[:, b, :])
            DMA("sp", out_d[own * SEG + t * 128: own * SEG + (t + 1) * 128, :], st, f"xo{t % 2}")

    def proj(sl, c0, ktn, act_fn, consume, nm=None):
        b = bank()
        o = ps[:, b, 0:SEG]
        MM(o, [(sl[:, kt, c0:c0 + 128], act_fn(kt)) for kt in range(ktn)])
        consume(o)

    deferred = []

    def flush_deferred():
        while deferred:
            deferred.pop(0)()

    def mixers(l, seg, full):
        P.label = "mix_in"
        p0 = seg * SEG
        nkt = (seg + 1) * NT
        win = w_in[l]
        xa = lambda kt: xT[:, kt, :]
        cosr, sinr, C2, S2 = tabs[:, 0, :], tabs[:, 1, :], tabs[:, 2, :], tabs[:, 3, :]
        sqs = [sview(0, [SEG], F32), sview(2048, [SEG], F32)]
        ta = sview(4096, [SEG], F32)
        tb = sview(6144, [SEG], F32)
        rstd = sview(8192, [SEG], F32)
        rq = sview(10240, [SEG], F32)
        ckvr = sview(12288, [2, SEG], F32)
        cq = sview(16384, [4, SEG], BF16)
        C2q = sview(20480, [SEG], F32)
        S2q = sview(22528, [SEG], F32)
        pst = ps[:, 6, 0:SEG]

        sl = load_slab([(512, 320)], KT, 0, win)
        if full:
            sl_q = load_slab([(0, 512)], KT, 0, win)
        flush_deferred()
        kr = sl[:, :, 256:320]
        CP("dve", sl[:, :, 384:448], kr)
        CP("dve", sl[:, :, 448:512], kr)
        kx4 = sl[:, :, 384:512].rearrange("p k (a b c) -> p k a b c", a=2, b=2)
        ky4 = sl[:, :, 256:384].rearrange("p k (a b c) -> p k a b c", a=2, b=2)
        CP("dve", ky4[:, :, :, 0, :], kx4[:, :, :, 1, :])
        CP("dve", ky4[:, :, :, 1, :], kx4[:, :, :, 0, :])
        for mt in range(2):
            def cons(o, mt=mt):
                CP("act", ckvr[:, mt, :], o)
                ACT(sqs[mt], o, AF.Square)
                P.op("pe", (lambda e: e.matmul(pst, lhsT=ones, rhs=sqs[mt], start=(mt == 0), stop=(mt == 1))),
                     [ones, sqs[mt]], [pst])
            proj(sl, mt * 128, KT, xa, cons)
        ACT(ta, pst, AF.Sqrt, bias=RMS_EPS, scale=1.0 / 256)
        RECIP(rstd, ta)
        for mt in range(2):
            STT("dve", ckvT[:, mt, p0:p0 + SEG], ckvr[:, mt, :], vec(f"kvg{l}", mt), rstd, ALU.mult, ALU.mult)
        bx, by = bank(), bank()
        MM(ps[:, bx, 0:SEG], [(sl[:, kt, 384:512], xa(kt)) for kt in range(KT)])
        MM(ps[:, by, 0:SEG], [(sl[:, kt, 256:384], xa(kt)) for kt in range(KT)])
        TT("dve", ta, ps[:, bx, 0:SEG], C2, ALU.mult)
        TT("dve", tb, ps[:, by, 0:SEG], S2, ALU.mult)
        TT("dve", kropeT[:, p0:p0 + SEG], ta, tb, ALU.add)

        if full:
            sl = sl_q
            for mt in range(4):
                def cons(o, mt=mt):
                    ACT(cq[:, mt, :], o, AF.Identity, scale=vec(f"qg{l}", mt))
                    ACT(sqs[mt % 2], o, AF.Square)
                    P.op("pe", (lambda e: e.matmul(pst, lhsT=ones, rhs=sqs[mt % 2], start=(mt == 0), stop=(mt == 3))),
                         [ones, sqs[mt % 2]], [pst])
                proj(sl, mt * 128, KT, xa, cons)
            ACT(ta, pst, AF.Sqrt, bias=RMS_EPS, scale=1.0 / 512)
            RECIP(rq, ta)
            TS("dve", rq, rq, MLA_SCALE, None, ALU.mult)
            TT("dve", C2q, C2, rq, ALU.mult)
            TT("dve", S2q, S2, rq, ALU.mult)

            wq = sview(24576, [4, 384], BF16)
            wxy = sview(27648, [4, 256], BF16)
            wkv = [sview(29696, [2, 256], BF16), sview(30720, [2, 256], BF16)]
            qn = [sview(31744, [SEG], BF16), sview(32768, [SEG], BF16)]
            qr = sview(33792, [SEG], BF16)
            knT = sview(34816, [KEYS], BF16)
            for pr in range(HM // 2):
                h0 = 2 * pr
                DMA("pool", wq, w_uq[l][:, h0 * 192:(h0 + 2) * 192].rearrange("(kt p) m -> p kt m", p=128), "wq")
                w3 = wq.rearrange("p k (h c) -> p k h c", c=192)
                x3 = wxy[:, :, 0:128].rearrange("p k (h c) -> p k h c", c=64)
                y3 = wxy[:, :, 128:256].rearrange("p k (h c) -> p k h c", c=64)
                CP("dve", x3, w3[:, :, :, 128:192])
                CP("dve", y3[:, :, :, 0:32], w3[:, :, :, 160:192])
                CP("dve", y3[:, :, :, 32:64], w3[:, :, :, 128:160])
                bx, by = bank(), bank()
                MM(ps[:, bx, 0:SEG], [(wxy[:, kt, 0:128], cq[:, kt, :]) for kt in range(4)])
                MM(ps[:, by, 0:SEG], [(wxy[:, kt, 128:256], cq[:, kt, :]) for kt in range(4)])
                TT("dve", ta, ps[:, bx, 0:SEG], C2q, ALU.mult)
                TT("dve", tb, ps[:, by, 0:SEG], S2q, ALU.mult)
                TT("dve", qr, ta, tb, ALU.add)
                for hh in range(2):
                    h = h0 + hh
                    bq = bank()
                    MM(ps[:, bq, 0:SEG], [(wq[:, kt, hh * 192:hh * 192 + 128], cq[:, kt, :]) for kt in range(4)])
                    TT("dve", qn[hh], ps[:, bq, 0:SEG], rq, ALU.mult)
                    mla_head(l, seg, h, hh, qn[hh], qr, wkv[hh], knT)

        ret_A(l, seg, 0, full)
        for h in range(HR):
            if h + 1 < HR:
                ret_A(l, seg, h + 1, full)
            ret_B(l, seg, h, full)

    Vt = P.sb("Vt", [128, NKT, 128], BF16)
    PTs = [P.sb(f"PT{i}", [128, SEG], BF16) for i in range(3)]
    pt_i = [0]

    def mla_head(l, seg, h, hh, qn, qr, wkv, knT):
        P.label = "mla_kv"
        p0 = seg * SEG
        nkt = (seg + 1) * NT
        nk = nkt * 128
        DMA("pool", wkv, w_ukv[l][:, h * 256:(h + 1) * 256].rearrange("(kt p) m -> p kt m", p=128), f"wkv{hh}")
        for g in range((nk + 511) // 512):
            n = min(512, nk - g * 512)
            b = bank()
            MM(ps[:, b, 0:n], [(wkv[:, ct, 0:128], ckvT[:, ct, g * 512:g * 512 + n]) for ct in range(2)])
            CP("act", knT[:, g * 512:g * 512 + n], ps[:, b, 0:n])
        for g in range(nkt // 4):
            b = bank()
            MMS([(ps[:, b, j * 128:(j + 1) * 128],
                  [(ckvT[:, ct, (4 * g + j) * 128:(4 * g + j + 1) * 128], wkv[:, ct, 128:256]) for ct in range(2)])
                 for j in range(4)])
            dst = Vt[:, 4 * g:4 * g + 4, :]
            src = ps[:, b, :].rearrange("p (j n) -> p j n", j=4)
            if 4 * g < NPRE * NT and seg >= NPRE:
                TS("dve", dst, src, flag[:, 0:1], None, ALU.mult)
            else:
                CP("dve", dst, src)
        P.label = "mla_att"
        pa, pd = ps[:, 6, 0:SEG], ps[:, 7, 0:SEG]
        base = 64 * hh
        pts = {}

        def score(kt):
            own_t = kt - seg * NT
            q0 = 0 if own_t < 0 else own_t * 128
            b = bank()
            sT = ps[:, b, q0:SEG]
            MM(sT, [(knT[:, kt * 128:(kt + 1) * 128], qn[:, q0:SEG]),
                    (kropeT[base:base + 64, kt * 128:(kt + 1) * 128], qr[base:base + 64, q0:SEG])])
            pt = PTs[pt_i[0] % 3]
            pt_i[0] += 1
            ACT(pt[:, q0:SEG], sT, AF.Exp)
            if own_t >= 0:
                TT("dve", pt[:, q0:q0 + 128], pt[:, q0:q0 + 128], maskb[:], ALU.mult)
            pts[kt] = (pt, q0, own_t)

        def pvacc(kt):
            pt, q0, own_t = pts.pop(kt)
            first = (kt == 0)
            den_l = flagb[:] if (kt < NPRE * NT and seg >= NPRE) else onesb[:]
            if own_t < 0:
                rngs = [(0, SEG, False)]
            else:
                rngs = [(q0, q0 + 128, True)] + ([(q0 + 128, SEG, False)] if q0 + 128 < SEG else [])

            def pv(e, kt=kt, pt=pt, rngs=rngs, first=first, den_l=den_l):
                ins = None
                for (a, b_, stp) in rngs:
                    e.matmul(pa[:, a:b_], lhsT=Vt[:, kt, :], rhs=pt[:, a:b_], start=first, stop=stp)
                    ins = e.matmul(pd[:, a:b_], lhsT=den_l, rhs=pt[:, a:b_], start=first, stop=stp)
                return ins
            P.op("pe", pv, [Vt[:, kt, :], pt[:, q0:SEG], den_l], [pa[:, q0:SEG], pd[:, q0:SEG]])

        for kt in range(min(2, nkt)):
            score(kt)
        for kt in range(nkt):
            pvacc(kt)
            if kt + 2 < nkt:
                score(kt + 2)
        rden = sview(4096, [SEG], F32)
        RECIP(rden, pd)
        TT("dve", mixT[:, h, :], pa, rden, ALU.mult)

    def ret_bufs(h):
        base = 8192 + (h % 2) * 10240
        return dict(qT=sview(base, [2, SEG], BF16), kT=sview(base + 2048, [2, SEG], BF16),
                    qd=sview(base + 4096, [2, SEG], BF16), vT=sview(base + 6144, [2, SEG], BF16),
                    sg=sview(base + 8192, [2, SEG], BF16))

    def ret_A(l, seg, h, full):
        P.label = "ret_proj"
        win = w_in[l]
        xa = lambda kt: xT[:, kt, :]
        cosr, sinr = tabs[:, 0, :], tabs[:, 1, :]
        t1 = sview(0, [SEG], F32)
        t2 = sview(2048, [SEG], F32)
        t3 = sview(4096, [SEG], F32)
        t4 = sview(6144, [SEG], F32)
        B = ret_bufs(h)
        qT, kT, qd, vT, sg = B["qT"], B["kT"], B["qd"], B["vT"], B["sg"]
        o_rq, o_rk, o_rv, o_rg = 832, 1856, 2880, 3904
        c_h = h * 256

        def rope(dst, b0, b1, scale):
            p1, p2 = ps[:, b0, 0:SEG], ps[:, b1, 0:SEG]
            TT("dve", t1, p1, cosr, ALU.mult)
            TT("dve", t2, p2, sinr, ALU.mult)
            TT("dve", t3, p2, cosr, ALU.mult)
            TT("dve", t4, p1, sinr, ALU.mult)
            TT("dve", t1, t1, t2, ALU.subtract)
            TT("dve", t3, t3, t4, ALU.add)
            ACT(dst[:, 0, :], t1, AF.Copy, scale=scale)
            ACT(dst[:, 1, :], t3, AF.Copy, scale=scale)

        if full:
            sl = load_slab([(o_rq + c_h, 256), (o_rk + c_h, 256)], KT, 0, win)
        else:
            sl = load_slab([(o_rk + c_h, 256), (o_rv + c_h, 256)], KT, 0, win)
        kc = 256 if full else 0
        if full:
            b0, b1 = bank(), bank()
            MM(ps[:, b0, 0:SEG], [(sl[:, kt, 0:128], xa(kt)) for kt in range(KT)])
            MM(ps[:, b1, 0:SEG], [(sl[:, kt, 128:256], xa(kt)) for kt in range(KT)])
            rope(qT, b0, b1, RET_SCALE)
            for dt_ in range(2):
                for t in range(NT):
                    TT("dve", qd[:, dt_, t * 128:(t + 1) * 128], qT[:, dt_, t * 128:(t + 1) * 128],
                       cst("qdec", 128 * h, 128 * (h + 1)), ALU.mult)
        b0, b1 = bank(), bank()
        MM(ps[:, b0, 0:SEG], [(sl[:, kt, kc:kc + 128], xa(kt)) for kt in range(KT)])
        MM(ps[:, b1, 0:SEG], [(sl[:, kt, kc + 128:kc + 256], xa(kt)) for kt in range(KT)])
        rope(kT, b0, b1, 1.0)
        if full:
            sl = load_slab([(o_rv + c_h, 256), (o_rg + c_h, 256)], KT, 0, win)
            vc = 0
        else:
            vc = 256
        for dt_ in range(2):
            b = bank()
            MM(ps[:, b, 0:SEG], [(sl[:, kt, vc + dt_ * 128:vc + (dt_ + 1) * 128], xa(kt)) for kt in range(KT)])
            CP("act", vT[:, dt_, :], ps[:, b, 0:SEG])
        if full:
            for dt_ in range(2):
                b = bank()
                MM(ps[:, b, 0:SEG], [(sl[:, kt, 256 + dt_ * 128:256 + (dt_ + 1) * 128], xa(kt)) for kt in range(KT)])
                ACT(sg[:, dt_, :], ps[:, b, 0:SEG], AF.Silu)

    def ret_B(l, seg, h, full):
        P.label = "ret_tr"
        lg = math.log1p(-2.0 ** (-5.0 - h))
        cdec = math.exp(lg * 128.0)
        B = ret_bufs(h)
        qT, kT, qd, vT, sg = B["qT"], B["kT"], B["qd"], B["vT"], B["sg"]
        kd = sview(28672, [NT, 256], BF16)
        vk = sview(30720, [NT, 256], BF16)
        of = sview(32768, [2, SEG], F32)
        sbf = [sview(36864 + 1024 * i, [2, 256], BF16) for i in range(4)]
        scT4 = sview(40960, [NT * 128], BF16)
        gm = sview(41984, [SEG], F32)
        gr = sview(44032, [SEG], F32)
        gt = sview(46080, [SEG], F32)
        for t in range(NT):
            bk, bv = bank(), bank()
            pk = ps[:, bk, 0:128].bitcast(BF16)
            pv = ps[:, bv, 0:128].bitcast(BF16)
            TR([(pk[:, dt_ * 128:(dt_ + 1) * 128], kT[:, dt_, t * 128:(t + 1) * 128]) for dt_ in range(2)], identb[:])
            TR([(pv[:, dt_ * 128:(dt_ + 1) * 128], vT[:, dt_, t * 128:(t + 1) * 128]) for dt_ in range(2)], identb[:])
            TS("dve", kd[:, t, :], pk, cst("kdec", h, h + 1), None, ALU.mult)
            CP("act", vk[:, t, :], pv)
        st = state[:, h, :, :]
        if seg == 0:
            MSET("dve", st, 0.0)
        P.label = "ret_scan"
        dth = cst("dt", 128 * h, 128 * (h + 1))
        if full:
            b = bank()
            MMS([(ps[:, b, t * 128:(t + 1) * 128],
                  [(kT[:, dt_, t * 128:(t + 1) * 128], qT[:, dt_, t * 128:(t + 1) * 128]) for dt_ in range(2)])
                 for t in range(NT)])
            for t in range(NT):
                TT("dve", scT4[:, t * 128:(t + 1) * 128], ps[:, b, t * 128:(t + 1) * 128], dth, ALU.mult)
        kvb = []
        for t in range(NT):
            b = bank()
            MMS([(ps[:, b, dt_ * 256:(dt_ + 1) * 256], [(kd[:, t, dt_ * 128:(dt_ + 1) * 128], vk[:, t, :])])
                 for dt_ in range(2)])
            kvb.append(b)
        for t in range(NT):
            if full:
                CP("act", sbf[t], st)
            STT("dve", st, st, cdec, ps[:, kvb[t], :].rearrange("p (a n) -> p a n", a=2), ALU.mult, ALU.add)
        if seg == NPRE - 1:
            TS("dve", st, st, flag[:, 0:1], None, ALU.mult)
        if not full:
            return
        for t2_ in range(0, NT, 2):
            b = bank()
            grp = []
            for t in range(t2_, min(t2_ + 2, NT)):
                ts_ = slice(t * 128, (t + 1) * 128)
                for dv in range(2):
                    o_ = ps[:, b, ((t - t2_) * 2 + dv) * 128:((t - t2_) * 2 + dv + 1) * 128]
                    grp.append((o_, [(vk[:, t, dv * 128:(dv + 1) * 128], scT4[:, ts_])]
                                + [(sbf[t][:, dt_, dv * 128:(dv + 1) * 128], qd[:, dt_, ts_]) for dt_ in range(2)]))
            MMS(grp)
            for t in range(t2_, min(t2_ + 2, NT)):
                src = ps[:, b, (t - t2_) * 256:(t - t2_ + 1) * 256].rearrange("p (a n) -> p a n", a=2)
                CP("act", of[:, :, t * 128:(t + 1) * 128], src)
        P.label = "ret_gn"
        pm, pq = ps[:, 6, 0:SEG], ps[:, 7, 0:SEG]
        MM(pm, [(ones, of[:, dv, :]) for dv in range(2)])
        for dv in range(2):
            ACT(gt, of[:, dv, :], AF.Square)
            P.op("pe", (lambda e, dv=dv: e.matmul(pq, lhsT=ones, rhs=gt, start=(dv == 0), stop=(dv == 1))),
                 [ones, gt], [pq])
        TS("dve", gm, pm, 1.0 / 256, None, ALU.mult)
        TT("dve", gt, gm, gm, ALU.mult)
        STT("dve", gt, pq, 1.0 / 256, gt, ALU.mult, ALU.subtract)
        ACT(gt, gt, AF.Sqrt, bias=GN_EPS, scale=1.0)
        RECIP(gr, gt)
        for dv in range(2):
            TT("dve", of[:, dv, :], of[:, dv, :], gm, ALU.subtract)
            TT("dve", of[:, dv, :], of[:, dv, :], gr, ALU.mult)
            ACT(of[:, dv, :], of[:, dv, :], AF.Identity, bias=vec(f"gnb{l}", 2 * h + dv), scale=vec(f"gng{l}", 2 * h + dv))
            TT("dve", mixT[:, 8 + 2 * h + dv, :], of[:, dv, :], sg[:, dv, :], ALU.mult)

    def out_proj(l):
        P.label = "out_proj"
        for g in range(4):
            sl = load_slab([(g * 512, 512)], KT, 0, w_out[l])
            for j in range(4):
                mt = 4 * g + j
                proj(sl, j * 128, KT, lambda kt: mixT[:, kt, :],
                     lambda o, mt=mt: STT("dve", xres[:, mt, :], xres[:, mt, :], ALPHA, o, ALU.mult, ALU.add))
                if mt >= 2:
                    stats_tile(mt - 2)
        stats_tile(KT - 2)
        stats_tile(KT - 1)

    def ffn(l):
        P.label = "ffn"
        hb = mixT
        parts = [(0, 12), (12, 10), (22, 12), (34, 10)]
        sgt = [sview(0, [SEG], F32), sview(2048, [SEG], F32)]
        for pi, (f0, nf) in enumerate(parts):
            P.label = "ffn_gu"
            for s2 in range(nf // 2):
                c0 = (f0 + 2 * s2) * 128
                sl = load_slab([(c0, 256)], KT, 0, w_gate[l])
                i = (slab_i[0] - 1) % NSLAB
                DMA("pool", sl[:, :, 256:512], w_up[l][:, c0:c0 + 256].rearrange("(kt p) m -> p kt m", p=128), f"w{i}_1")
                for j in range(2):
                    bg, bu = bank(), bank()
                    MM(ps[:, bg, 0:SEG], [(sl[:, kt, j * 128:(j + 1) * 128], xT[:, kt, :]) for kt in range(KT)])
                    MM(ps[:, bu, 0:SEG], [(sl[:, kt, 256 + j * 128:256 + (j + 1) * 128], xT[:, kt, :]) for kt in range(KT)])
                    s_ = sgt[j % 2]
                    ACT(s_, ps[:, bg, 0:SEG], AF.Silu)
                    TT("dve", hb[:, 2 * s2 + j, :], ps[:, bu, 0:SEG], s_, ALU.mult)
            P.label = "ffn_down"
            for g in range(4):
                sl = load_slab([(g * 512, 512)], nf, f0 * 128, w_down[l])
                for j in range(4):
                    mt = 4 * g + j
                    if pi == 0:
                        cons = lambda o, mt=mt: STT("dve", xres[:, mt, :], xres[:, mt, :], ALPHA, o, ALU.mult, ALU.add)
                    else:
                        cons = lambda o, mt=mt: TT("dve", xres[:, mt, :], xres[:, mt, :], o, ALU.add)
                    proj(sl, j * 128, nf, lambda kt: hb[:, kt, :], cons)
                    if pi == len(parts) - 1 and mt >= 2:
                        stats_tile(mt - 2)
        stats_tile(KT - 2)
        stats_tile(KT - 1)

    def full_layer(l, seg, prefetch=None, after_kt=None, rope_next=None):
        mixers(l, seg, True)
        if rope_next is not None:
            rope_tables(rope_next)
        if prefetch is not None:
            prefetch_x(prefetch)
        out_proj(l)
        layer_norm(f"ln1_g{l}", f"ln1_b{l}", fused_stats=True)
        ffn(l)
        layer_norm(f"ln2_g{l}", f"ln2_b{l}", fused_stats=True, after_kt=after_kt, write_xT=(l < last))

    last = n_layers - 1
    PAIRS = [[2 * i, 2 * i + 1] for i in range(n_pairs)]
    def nxt(l, seg):
        if seg + 1 < NSEG:
            return seg + 1
        return 0 if l < last else None

    rope_tables(0)
    for seg in range(NSEG):
        own = seg - NPRE
        load_x(seg)
        if own < 0:
            mixers(0, seg, False)
            rope_tables(seg + 1)
            prefetch_x(seg + 1)
            continue
        if last == 0:
            full_layer(0, seg, prefetch=seg + 1, rope_next=nxt(0, seg))
            store_out(own)
            continue

        def spill(kt, own=own):
            if kt % 4 != 3:
                return
            c = kt // 4
            cols = slice(4 * c * SEG, (4 * c + 4) * SEG)
            DMA("sp", spb[own][:, cols].rearrange("p (k n) -> p k n", k=4), xT[:, 4 * c:4 * c + 4, :], "spl")
            DMA("sp", spf[own][:, cols].rearrange("p (k n) -> p k n", k=4), xres[:, 4 * c:4 * c + 4, :], "spl2")
        full_layer(0, seg, prefetch=seg + 1, after_kt=spill, rope_next=nxt(0, seg))
        def coll(own=own, need=P.cnt["spl"]):
            P.op("pool", lambda e: e.collective_compute("AllGather", ALU.bypass, replica_groups=PAIRS,
                                                         ins=[spb[own]], outs=[gath[own]]),
                 [], [], dma="cc", inc=1, extra=[("spl", need)])
        deferred.append(coll)
    if last >= 1:
        for seg in range(NSEG):
            own = seg - NPRE
            if seg == 1:
                flush_deferred()
            if own < 0:
                DMA("sp", xT[:], gath[seg][0:128, :].rearrange("p (k n) -> p k n", k=KT), "spl",
                    extra=[("cc", seg + 1)])
                mixers(1, seg, False)
                rope_tables(seg + 1)
            else:
                DMA("sp", xT[:], spb[own].rearrange("p (k n) -> p k n", k=KT), "spl")
                DMA("sp", xres[:], spf[own].rearrange("p (k n) -> p k n", k=KT), "spl2")
                full_layer(1, seg, rope_next=nxt(1, seg))
                store_out(own)

    import os as _os
    if _os.environ.get("KDBG_WAITLAB"):
        import json as _json
        _json.dump(dict(names=P.names, lab=P.waitlab), open(_os.environ["KDBG_WAITLAB"], "w"))
    P.emit(["xo0", "xo1"])
    stack.close()
    return nc


_NC_CACHE = {}


def _get_nc(SEG, NPRE, NOWN, n_layers=DEPTH, n_pairs=4):
    key = (SEG, NPRE, NOWN, n_layers, n_pairs)
    if key not in _NC_CACHE:
        _NC_CACHE[key] = build(SEG, NPRE, NOWN, n_layers, n_pairs)
    return _NC_CACHE[key]


def run_cores(inp, SEG, n_layers=DEPTH, cores=None):
    x = np.asarray(inp["x"], np.float32)
    posn = np.asarray(inp["positions"], np.int32)
    B, S, _ = x.shape
    half = S // 2
    NPRE = NOWN = half // SEG
    nc = _get_nc(SEG, NPRE, NOWN, n_layers, B)
    vecs = pack_vecs(inp)
    cstv = pack_cst()
    wnames = ("w_in", "w_uq", "w_ukv", "w_out", "w_gate", "w_up", "w_down")
    W = {k: np.ascontiguousarray(np.asarray(inp[k], np.float32)) for k in wnames}
    in_maps = []
    ids = []
    for b in range(B):
        for h in range(2):
            if cores is not None and (b, h) not in cores:
                continue
            if h == 1:
                xi, pi = x[b], posn[b]
            else:
                xi = np.concatenate([x[b, :half], x[b, :half]], axis=0)
                pi = np.concatenate([posn[b, :half], posn[b, :half]], axis=0)
            m = dict(xin=np.ascontiguousarray(xi), pos=np.ascontiguousarray(pi),
                     flag=np.full((128, 1), float(h), np.float32), vecs=vecs, cst=cstv)
            m.update(W)
            in_maps.append(m)
            ids.append((b, h))
    res = run_bass_kernel_spmd(nc, in_maps, core_ids=list(range(len(in_maps))))
    out = np.zeros((B, S, D), np.float32)
    for (b, h), r in zip(ids, res.results):
        out[b, h * half:(h + 1) * half] = r["out"]
    return out


def kernel(**inputs):
    return run_cores(inputs, SEG=512)
```

```python
import math
from contextlib import ExitStack

import numpy as np
import concourse.bass as bass
import concourse.mybir as mybir
from concourse.bass_utils import run_bass_kernel_spmd

F32 = mybir.dt.float32
BF16 = mybir.dt.bfloat16
I32 = mybir.dt.int32
AF = mybir.ActivationFunctionType
ALU = mybir.AluOpType

D = 2048
KT = 16
DIN = 4928
DFF = 5632
DEPTH = 2
HM = 8
HR = 4
LN_EPS = 1e-5
RMS_EPS = 1e-6
GN_EPS = 1e-5
ALPHA = (2 * DEPTH) ** 0.25
MLA_SCALE = 192 ** -0.5
RET_SCALE = 256 ** -0.5
TWO_PI = 2.0 * math.pi
CW1 = 6.28125
CW2 = TWO_PI - CW1
ENGS = ("pe", "act", "dve", "pool", "sp")
ESZ = {F32: 4, BF16: 2, I32: 4}


def esize(dt):
    return ESZ[dt]


class Prog:
    CELL = 64
    MAXSRC = 48

    def __init__(self, nc, stack):
        self.nc = nc
        self.stack = stack
        self.src_idx = {}
        self.names = []
        self.sems = {}
        self.cnt = {}
        self.ops = {e: [] for e in ENGS}
        self.seen = {e: {} for e in ENGS}
        self.trk = {}
        self.cache = {}
        self.label = ""
        self.waitlab = {}
        for e in ("pe", "act", "dve", "pool"):
            self.add_src(e)

    def add_src(self, name):
        if name in self.src_idx:
            return name
        self.src_idx[name] = len(self.names)
        self.names.append(name)
        assert len(self.names) <= self.MAXSRC
        self.sems[name] = self.stack.enter_context(self.nc.semaphore("s_" + name))
        self.cnt[name] = 0
        return name

    def _reg(self, name, nbytes):
        n = (nbytes + self.CELL - 1) // self.CELL
        self.trk[name] = dict(ws=np.full(n, -1, np.int64), wv=np.zeros(n, np.int64),
                              rv=np.zeros((self.MAXSRC, n), np.int64))

    def sb(self, name, shape, dt):
        t = self.stack.enter_context(self.nc.sbuf_tensor(name, list(shape), dt))
        self._reg(name, int(np.prod(shape[1:])) * esize(dt))
        return t

    def psum(self, name, shape, dt):
        t = self.stack.enter_context(self.nc.psum_tensor(name, list(shape), dt))
        self._reg(name, int(np.prod(shape[1:])) * esize(dt))
        return t

    def cells(self, ap):
        name = ap.tensor.name
        T = self.trk.get(name)
        if T is None:
            return None
        pat = ap.ap
        key = (name, int(ap.offset), pat, str(ap.dtype))
        c = self.cache.get(key)
        if c is not None:
            return T, c
        es = esize(ap.dtype)
        pstep = pat[0][0]
        off = int(ap.offset) % pstep if pstep > 0 else int(ap.offset)
        starts = np.array([off], np.int64)
        free = [p for p in pat[1:]]
        if free:
            for st, cn in free[:-1]:
                starts = (starts[:, None] + st * np.arange(cn, dtype=np.int64)[None, :]).ravel()
            lst, lcn = free[-1]
            run = (lcn - 1) * abs(lst) + 1
        else:
            run = 1
        lo = starts * es // self.CELL
        hi = ((starts + run) * es - 1) // self.CELL
        if len(lo) == 1:
            idx = np.arange(lo[0], hi[0] + 1)
        else:
            idx = np.unique(np.concatenate([np.arange(a, b + 1) for a, b in zip(lo, hi)]))
        self.cache[key] = idx
        return T, idx

    def op(self, eng, fn, reads=(), writes=(), dma=None, inc=None, extra=()):
        src = dma if dma else eng
        deps = {}
        for s_, v_ in extra:
            if v_ > 0:
                deps[s_] = max(deps.get(s_, 0), v_)

        def need(s, v):
            if v > deps.get(s, 0):
                deps[s] = v

        rc = [self.cells(a) for a in reads]
        wc = [self.cells(a) for a in writes]
        for c in rc:
            if c is None:
                continue
            T, idx = c
            ws = T["ws"][idx]
            wv = T["wv"][idx]
            for s in np.unique(ws):
                if s < 0:
                    continue
                nm = self.names[s]
                if nm == "pe" and eng == "pe" and not dma:
                    continue
                need(nm, int(wv[ws == s].max()))
        for c in wc:
            if c is None:
                continue
            T, idx = c
            ws = T["ws"][idx]
            wv = T["wv"][idx]
            for s in np.unique(ws):
                if s < 0:
                    continue
                nm = self.names[s]
                if nm == src and not dma:
                    continue
                need(nm, int(wv[ws == s].max()))
            rv = T["rv"][:, idx].max(axis=1)
            for s in np.nonzero(rv)[0]:
                nm = self.names[s]
                if nm == src and not dma:
                    continue
                need(nm, int(rv[s]))
        if dma:
            if self.cnt[src] > 0:
                need(src, self.cnt[src])
        if inc is None:
            inc = 16 if dma else 1
        self.cnt[src] += inc
        val = self.cnt[src]
        si = self.src_idx[src]
        for c in rc:
            if c is not None:
                T, idx = c
                T["rv"][si, idx] = val
        for c in wc:
            if c is not None:
                T, idx = c
                T["ws"][idx] = si
                T["wv"][idx] = val
                T["rv"][:, idx] = 0
        seen = self.seen[eng]
        waits = []
        for s, v in deps.items():
            if v > seen.get(s, 0):
                waits.append((s, v))
                seen[s] = v
                if eng == "pe":
                    self.waitlab[f"{self.src_idx[s]}:{v}"] = self.label
        self.ops[eng].append((waits, fn, src, inc))

    def emit(self, final_waits):
        nc = self.nc
        block = self.stack.enter_context(nc.Block())

        def replay(name):
            def run(e):
                for waits, fn, src, inc in self.ops[name]:
                    for s, v in waits:
                        e.wait_ge(self.sems[s], v)
                    ins = fn(e)
                    ins.then_inc(self.sems[src], inc)
                if name == "sp":
                    for s in final_waits:
                        e.wait_ge(self.sems[s], self.cnt[s])
            return run

        block.tensor(replay("pe"))
        block.scalar(replay("act"))
        block.vector(replay("dve"))
        block.gpsimd(replay("pool"))
        block.sync(replay("sp"))


def _fm(v):
    return np.ascontiguousarray(np.asarray(v, np.float32).reshape(-1, 128).T)


VEC_LAYOUT = {}


def _vec_layout():
    if VEC_LAYOUT:
        return VEC_LAYOUT
    off = 0

    def add(name, n):
        nonlocal off
        VEC_LAYOUT[name] = (off, n)
        off += n

    add("ln_in_g", 16)
    add("ln_in_b", 16)
    for l in range(DEPTH):
        for nm in ("ln1_g", "ln1_b", "ln2_g", "ln2_b"):
            add(f"{nm}{l}", 16)
        add(f"qg{l}", 4)
        add(f"kvg{l}", 2)
        add(f"gng{l}", 8)
        add(f"gnb{l}", 8)
    VEC_LAYOUT["_n"] = (off, 0)
    return VEC_LAYOUT


def pack_vecs(inp):
    L = _vec_layout()
    out = np.zeros((128, L["_n"][0]), np.float32)

    def put(name, v):
        o, n = L[name]
        out[:, o:o + n] = _fm(v)

    put("ln_in_g", inp["ln_in_g"])
    put("ln_in_b", inp["ln_in_b"])
    for l in range(DEPTH):
        put(f"ln1_g{l}", inp["ln1_g"][l])
        put(f"ln1_b{l}", inp["ln1_b"][l])
        put(f"ln2_g{l}", inp["ln2_g"][l])
        put(f"ln2_b{l}", inp["ln2_b"][l])
        put(f"qg{l}", inp["q_norm_g"][l])
        put(f"kvg{l}", inp["kv_norm_g"][l])
        put(f"gng{l}", inp["ret_gn_g"][l])
        put(f"gnb{l}", inp["ret_gn_b"][l])
    return out


CST = dict(ident=(0, 128), ones=(128, 128), mask=(256, 128), dt=(384, 512), qdec=(896, 512),
           kdec=(1408, 4), invf_r=(1412, 1), invf_m=(1413, 1), sgn_m=(1414, 1), _n=(1415, 0))


def pack_cst():
    c = np.zeros((128, CST["_n"][0]), np.float64)
    idx = np.arange(128)
    c[:, 0:128] = np.eye(128)
    c[:, 128:256] = 1.0
    ch = idx // 64
    c[:, 256:384] = (ch[:, None] <= ch[None, :]).astype(np.float64)
    for h in range(HR):
        lg = math.log1p(-2.0 ** (-5.0 - h))
        dist = np.abs(idx[:, None] - idx[None, :])
        c[:, 384 + 128 * h:384 + 128 * (h + 1)] = np.exp(lg * dist) * (ch[:, None] <= ch[None, :])
        c[:, 896 + 128 * h:896 + 128 * (h + 1)] = np.exp(lg * (idx + 1.0))[None, :]
        c[:, 1408 + h] = np.exp(lg * (127.0 - idx))
    c[:, 1412] = 10000.0 ** (-np.arange(0, 256, 2, dtype=np.float64) / 256.0)
    invm = (10000.0 ** (-np.arange(0, 64, 2, dtype=np.float32) / np.float32(64))).astype(np.float64)
    c[:, 1413] = np.tile(invm, 4)
    c[:, 1414] = np.tile(np.concatenate([-np.ones(32), np.ones(32)]), 2)
    c = c.astype(np.float32)
    c[:, 1412] = (np.float32(10000.0) ** (-np.arange(0, 256, 2, dtype=np.float32) / np.float32(256))).astype(np.float32)
    return c


def build(SEG, NPRE, NOWN, n_layers=DEPTH, n_pairs=4):
    nc = bass.Bass("TRN2", target_bir_lowering=False)
    NSEG = NPRE + NOWN
    KEYS = NSEG * SEG
    NT = SEG // 128
    NKT = KEYS // 128
    VL = _vec_layout()

    def dram(name, shape, dt, kind="ExternalInput"):
        return nc.dram_tensor(name, list(shape), dt, kind=kind).ap()

    xin = dram("xin", [NSEG * SEG, D], F32)
    pos_d = dram("pos", [NSEG * SEG], I32)
    flag_d = dram("flag", [128, 1], F32)
    vecs_d = dram("vecs", [128, VL["_n"][0]], F32)
    cst_d = dram("cst", [128, CST["_n"][0]], F32)
    w_in = dram("w_in", [DEPTH, D, DIN], F32)
    w_uq = dram("w_uq", [DEPTH, 512, 1536], F32)
    w_ukv = dram("w_ukv", [DEPTH, 256, 2048], F32)
    w_out = dram("w_out", [DEPTH, D, D], F32)
    w_gate = dram("w_gate", [DEPTH, D, DFF], F32)
    w_up = dram("w_up", [DEPTH, D, DFF], F32)
    w_down = dram("w_down", [DEPTH, DFF, D], F32)
    out_d = dram("out", [NOWN * SEG, D], F32, kind="ExternalOutput")
    spf = dram("spf", [NOWN, 128, KT * SEG], F32, kind="Internal")
    spb = [dram(f"spb{i}", [128, KT * SEG], BF16, kind="Internal") for i in range(NOWN)]
    gath = [dram(f"gath{i}", [2 * 128, KT * SEG], BF16, kind="Internal") for i in range(NOWN)]

    stack = ExitStack()
    P = Prog(nc, stack)

    xres = P.sb("xres", [128, KT, SEG], F32)
    xT = P.sb("xT", [128, KT, SEG], BF16)
    mixT = P.sb("mixT", [128, KT, SEG], BF16)
    ckvT = P.sb("ckvT", [128, 2, KEYS], BF16)
    kropeT = P.sb("kropeT", [128, KEYS], BF16)
    state = P.sb("state", [128, HR, 2, 256], F32)
    NSLAB = 2
    slabA = P.sb("slabA", [128, NSLAB * KT * 512], BF16)
    slabs = [slabA[:, i * KT * 512:(i + 1) * KT * 512].rearrange("p (k n) -> p k n", k=KT) for i in range(NSLAB)]
    NSLAB3 = 3
    slabs3 = [slabA[:, i * 12 * 384:(i + 1) * 12 * 384].rearrange("p (k n) -> p k n", k=12) for i in range(NSLAB3)]
    for i in range(3):
        P.add_src(f"w{i}_0")
        P.add_src(f"w{i}_1")
    cstf = P.sb("cstf", [128, CST["_n"][0]], F32)
    vecs = P.sb("vecs_sb", [128, VL["_n"][0]], F32)
    flag = P.sb("flag_sb", [128, 1], F32)
    identb = P.sb("identb", [128, 128], BF16)
    onesb = P.sb("onesb", [128, 128], BF16)
    flagb = P.sb("flagb", [128, 128], BF16)
    maskb = P.sb("maskb", [128, 128], BF16)
    tabs = P.sb("tabs", [128, 4, SEG], F32)
    posi = P.sb("posi", [128, SEG], I32)
    SCR = 48 * 1024
    scr = P.sb("scr", [128, SCR // 2], BF16)
    for s in ("c0", "xi0", "xi1", "xo0", "xo1", "spl", "spl2", "wq", "wkv0", "wkv1", "pp", "cc"):
        P.add_src(s)

    ps = P.psum("ps", [128, 8, 512], F32)

    ident = cstf[:, 0:128]
    ones = cstf[:, 128:256]

    def cst(name, a=0, b=None):
        o, n = CST[name]
        return cstf[:, o + a:o + (n if b is None else b)]

    def vec(name, k):
        o, n = VL[name]
        return vecs[:, o + k:o + k + 1]

    def sview(off, shape, dt):
        n = int(np.prod(shape))
        es = esize(dt)
        assert off % 4 == 0 and off + n * es <= SCR, (off, shape)
        v = scr[:, off // 2: off // 2 + n * es // 2]
        if dt != BF16:
            v = v.bitcast(dt)
        if len(shape) == 2:
            return v.rearrange("p (a b) -> p a b", a=shape[0])
        if len(shape) == 3:
            return v.rearrange("p (a b c) -> p a b c", a=shape[0], b=shape[1])
        return v

    rot = [0]

    def bank():
        b = rot[0]
        rot[0] = (b + 1) % 6
        return b

    def MM(out, pairs, eng="pe"):
        reads = []
        for l, r in pairs:
            reads += [l, r]
        n = len(pairs)

        def fn(e):
            ins = None
            for i, (l, r) in enumerate(pairs):
                ins = e.matmul(out, lhsT=l, rhs=r, start=(i == 0), stop=(i == n - 1))
            return ins
        P.op("pe", fn, reads, [out])

    def MMS(groups):
        reads, writes = [], []
        for o, pairs in groups:
            writes.append(o)
            for l, r in pairs:
                reads += [l, r]

        def fn(e):
            ins = None
            for o, pairs in groups:
                n = len(pairs)
                for i, (l, r) in enumerate(pairs):
                    ins = e.matmul(o, lhsT=l, rhs=r, start=(i == 0), stop=(i == n - 1))
            return ins
        P.op("pe", fn, reads, writes)

    def TR(pairs, idn):
        reads = [idn] + [i for _, i in pairs]
        writes = [o for o, _ in pairs]

        def fn(e):
            ins = None
            for o, i in pairs:
                ins = e.transpose(o, i, idn)
            return ins
        P.op("pe", fn, reads, writes)

    def ACT(out, in_, func, bias=None, scale=None, eng="act"):
        reads = [in_]
        kw = {}
        if bias is not None:
            kw["bias"] = bias
            if not isinstance(bias, float):
                reads.append(bias)
        if scale is not None:
            kw["scale"] = scale
            if not isinstance(scale, float):
                reads.append(scale)
        P.op("act", lambda e: e.activation(out=out, in_=in_, func=func, **kw), reads, [out])

    def TT(eng, out, in0, in1, op):
        f = lambda e: e.tensor_tensor(out=out, in0=in0, in1=in1, op=op)
        P.op(eng, f, [in0, in1], [out])

    def TS(eng, out, in0, s1, s2, op0, op1=None):
        reads = [in0] + [s for s in (s1, s2) if s is not None and not isinstance(s, float)]
        if op1 is None:
            f = lambda e: e.tensor_scalar(out=out, in0=in0, scalar1=s1, scalar2=None, op0=op0)
        else:
            f = lambda e: e.tensor_scalar(out=out, in0=in0, scalar1=s1, scalar2=s2, op0=op0, op1=op1)
        P.op(eng, f, reads, [out])

    def STT(eng, out, in0, scalar, in1, op0, op1):
        reads = [in0, in1] + ([] if isinstance(scalar, float) else [scalar])
        f = lambda e: e.scalar_tensor_tensor(out=out, in0=in0, scalar=scalar, in1=in1, op0=op0, op1=op1)
        P.op(eng, f, reads, [out])

    def CP(eng, out, in_):
        if eng == "act":
            ACT(out, in_, AF.Copy)
        else:
            P.op(eng, lambda e: e.tensor_copy(out=out, in_=in_), [in_], [out])

    def MSET(eng, out, val):
        P.op(eng, lambda e: e.memset(out, val), [], [out])

    def RECIP(out, in_):
        P.op("dve", lambda e: e.reciprocal(out=out, in_=in_), [in_], [out])

    def DMA(q, out, in_, sem, extra=()):
        P.op(q, lambda e: e.dma_start(out=out, in_=in_), [in_], [out], dma=sem, extra=extra)

    DMA("sp", cstf[:], cst_d, "c0")
    DMA("sp", vecs[:], vecs_d, "c0")
    DMA("sp", flag[:], flag_d, "c0")
    CP("dve", identb[:], ident)
    CP("dve", onesb[:], ones)
    CP("dve", maskb[:], cst("mask"))
    TS("dve", flagb[:], ones, flag[:, 0:1], None, ALU.mult)

    slab_i = [0]

    slab3_i = [0]

    def load_slab(runs, kt_n, row0, wsrc, narrow=False):
        if narrow:
            i = slab3_i[0] % NSLAB3
            slab3_i[0] += 1
            sl = slabs3[i]
        else:
            i = slab_i[0] % NSLAB
            slab_i[0] += 1
            sl = slabs[i]
        c = 0
        for r_, (c0, n) in enumerate(runs):
            src = wsrc[row0:row0 + kt_n * 128, c0:c0 + n].rearrange("(kt p) m -> p kt m", p=128)
            DMA("pool", sl[:, 0:kt_n, c:c + n], src, f"w{i}_{r_ % 2}")
            c += n
        return sl

    accx = sview(10240, [SEG], F32)
    accq = sview(12288, [SEG], F32)

    def stats_tile(mt):
        sqb = sview(14336 + 0, [SEG], F32) if False else sview(8192, [SEG], F32)
        if mt == 0:
            CP("dve", accx, xres[:, mt, :])
            ACT(accq, xres[:, mt, :], AF.Square)
        else:
            TT("dve", accx, accx, xres[:, mt, :], ALU.add)
            ACT(sqb, xres[:, mt, :], AF.Square)
            TT("dve", accq, accq, sqb, ALU.add)

    def layer_norm(gname, bname, write_xT=True, fused_stats=False, after_kt=None):
        P.label = "ln"
        sq = [sview(0, [SEG], F32), sview(2048, [SEG], F32)]
        mean = sview(4096, [SEG], F32)
        rstd = sview(6144, [SEG], F32)
        tmp = sview(8192, [SEG], F32)
        pm, pq = ps[:, 6, 0:SEG], ps[:, 7, 0:SEG]
        if fused_stats == "psum":
            pass
        elif fused_stats:
            MM(pm, [(ones, accx)])
            MM(pq, [(ones, accq)])
        else:
            MM(pm, [(ones, xres[:, kt, :]) for kt in range(KT)])
            for kt in range(KT):
                ACT(sq[kt % 2], xres[:, kt, :], AF.Square)
                P.op("pe", (lambda e, kt=kt: e.matmul(pq, lhsT=ones, rhs=sq[kt % 2], start=(kt == 0), stop=(kt == KT - 1))),
                     [ones, sq[kt % 2]], [pq])
        TS("dve", mean, pm, 1.0 / D, None, ALU.mult)
        TT("dve", tmp, mean, mean, ALU.mult)
        STT("dve", tmp, pq, 1.0 / D, tmp, ALU.mult, ALU.subtract)
        ACT(tmp, tmp, AF.Sqrt, bias=LN_EPS, scale=1.0)
        RECIP(rstd, tmp)
        for kt in range(KT):
            TT("dve", xres[:, kt, :], xres[:, kt, :], mean, ALU.subtract)
            TT("dve", xres[:, kt, :], xres[:, kt, :], rstd, ALU.mult)
            if write_xT:
                ACT(xT[:, kt, :], xres[:, kt, :], AF.Identity, bias=vec(bname, kt), scale=vec(gname, kt))
        for kt in range(KT):
            ACT(xres[:, kt, :], xres[:, kt, :], AF.Identity, bias=vec(bname, kt), scale=vec(gname, kt))
            if after_kt is not None:
                after_kt(kt)

    def rope_tables(seg):
        DMA("sp", posi[:], pos_d[seg * SEG:(seg + 1) * SEG].partition_broadcast(128), "pp")
        posf = sview(0, [SEG], F32)
        ang = sview(2048, [SEG], F32)
        ki = sview(4096, [SEG], I32)
        kf = sview(6144, [SEG], F32)
        CP("dve", posf, posi[:])
        for which, invf, sgn in ((0, cst("invf_r"), None), (1, cst("invf_m"), cst("sgn_m"))):
            rc = tabs[:, 2 * which, :]
            r = tabs[:, 2 * which + 1, :]
            m = kf
            TS("dve", ang, posf, invf, None, ALU.mult)
            TS("dve", ki, ang, 1.0 / TWO_PI, None, ALU.mult)
            CP("dve", kf, ki)
            STT("dve", r, kf, -CW1, ang, ALU.mult, ALU.add)
            STT("dve", r, kf, -CW2, r, ALU.mult, ALU.add)
            TS("dve", r, r, -math.pi, math.pi, ALU.max, ALU.min)
            TS("dve", m, r, math.pi / 2, -TWO_PI, ALU.is_gt, ALU.mult)
            STT("dve", rc, r, math.pi / 2, m, ALU.add, ALU.add)
            TS("dve", rc, rc, -math.pi, math.pi, ALU.max, ALU.min)
            ACT(rc, rc, AF.Sin)
            if sgn is None:
                ACT(r, r, AF.Sin)
            else:
                ACT(r, r, AF.Sin, scale=sgn)

    xstg = [sview(16384 + 8192 * t, [D], F32) for t in range(4)]
    prefetched = set()

    def prefetch_x(seg):
        if seg >= NSEG or NT > 4:
            return
        for t in range(NT):
            DMA("sp", xstg[t], xin[seg * SEG + t * 128: seg * SEG + (t + 1) * 128, :], f"xi{t % 2}")
        prefetched.add(seg)

    def load_x(seg):
        P.label = "load_x"
        if seg not in prefetched:
            prefetch_x(seg)
        pm, pq = ps[:, 6, 0:SEG], ps[:, 7, 0:SEG]
        sqs = [sview(0, [SEG], F32), sview(2048, [SEG], F32)]

        def stats_g(g):
            for kt in range(4 * g, 4 * g + 4):
                sqb = sqs[kt % 2]
                ACT(sqb, xres[:, kt, :], AF.Square)

                def fn(e, kt=kt, sqb=sqb):
                    e.matmul(pm, lhsT=ones, rhs=xres[:, kt, :], start=(kt == 0), stop=(kt == KT - 1))
                    return e.matmul(pq, lhsT=ones, rhs=sqb, start=(kt == 0), stop=(kt == KT - 1))
                P.op("pe", fn, [ones, xres[:, kt, :], sqb], [pm, pq])

        for g in range(4):
            for t in range(NT):
                st = xstg[t]
                b = bank()
                TR([(ps[:, b, j * 128:(j + 1) * 128], st[:, (4 * g + j) * 128:(4 * g + j + 1) * 128]) for j in range(4)], ident)
                src = ps[:, b, :].rearrange("p (j n) -> p j n", j=4)
                CP("dve" if t % 2 == 0 else "act", xres[:, 4 * g:4 * g + 4, t * 128:(t + 1) * 128], src)
            if g >= 1:
                stats_g(g - 1)
        stats_g(3)
        layer_norm("ln_in_g", "ln_in_b", fused_stats="psum")

    def store_out(own):
        P.label = "store"
        stg = [sview(16384, [D], F32), sview(24576, [D], F32)]
        for t in range(NT):
            st = stg[t % 2]
            for g in range(4):
                b = bank()
                TR([(ps[:, b, j * 128:(j + 1) * 128], xres[:, 4 * g + j, t * 128:(t + 1) * 128]) for j in range(4)], ident)
                CP("dve" if g % 2 == 0 else "act", st[:, g * 512:(g + 1) * 512], ps[:, b, :])
            DMA("sp", out_d[own * SEG + t * 128: own * SEG + (t + 1) * 128, :], st, f"xo{t % 2}")

    def proj(sl, c0, ktn, act_fn, consume, nm=None):
        b = bank()
        o = ps[:, b, 0:SEG]
        MM(o, [(sl[:, kt, c0:c0 + 128], act_fn(kt)) for kt in range(ktn)])
        consume(o)

    deferred = []

    def flush_deferred():
        while deferred:
            deferred.pop(0)()

    def mixers(l, seg, full):
        P.label = "mix_in"
        p0 = seg * SEG
        nkt = (seg + 1) * NT
        win = w_in[l]
        xa = lambda kt: xT[:, kt, :]
        cosr, sinr, C2, S2 = tabs[:, 0, :], tabs[:, 1, :], tabs[:, 2, :], tabs[:, 3, :]
        sqs = [sview(0, [SEG], F32), sview(2048, [SEG], F32)]
        ta = sview(4096, [SEG], F32)
        tb = sview(6144, [SEG], F32)
        rstd = sview(8192, [SEG], F32)
        rq = sview(10240, [SEG], F32)
        ckvr = sview(12288, [2, SEG], F32)
        cq = sview(16384, [4, SEG], BF16)
        C2q = sview(20480, [SEG], F32)
        S2q = sview(22528, [SEG], F32)
        pst = ps[:, 6, 0:SEG]

        sl = load_slab([(512, 320)], KT, 0, win)
        kr = sl[:, :, 256:320]
        CP("dve", sl[:, :, 384:448], kr)
        CP("dve", sl[:, :, 448:512], kr)
        kx4 = sl[:, :, 384:512].rearrange("p k (a b c) -> p k a b c", a=2, b=2)
        ky4 = sl[:, :, 256:384].rearrange("p k (a b c) -> p k a b c", a=2, b=2)
        CP("dve", ky4[:, :, :, 0, :], kx4[:, :, :, 1, :])
        CP("dve", ky4[:, :, :, 1, :], kx4[:, :, :, 0, :])
        for mt in range(2):
            def cons(o, mt=mt):
                CP("act", ckvr[:, mt, :], o)
                ACT(sqs[mt], o, AF.Square)
                P.op("pe", (lambda e: e.matmul(pst, lhsT=ones, rhs=sqs[mt], start=(mt == 0), stop=(mt == 1))),
                     [ones, sqs[mt]], [pst])
            proj(sl, mt * 128, KT, xa, cons)
        ACT(ta, pst, AF.Sqrt, bias=RMS_EPS, scale=1.0 / 256)
        RECIP(rstd, ta)
        for mt in range(2):
            STT("dve", ckvT[:, mt, p0:p0 + SEG], ckvr[:, mt, :], vec(f"kvg{l}", mt), rstd, ALU.mult, ALU.mult)
        bx, by = bank(), bank()
        MM(ps[:, bx, 0:SEG], [(sl[:, kt, 384:512], xa(kt)) for kt in range(KT)])
        MM(ps[:, by, 0:SEG], [(sl[:, kt, 256:384], xa(kt)) for kt in range(KT)])
        TT("dve", ta, ps[:, bx, 0:SEG], C2, ALU.mult)
        TT("dve", tb, ps[:, by, 0:SEG], S2, ALU.mult)
        TT("dve", kropeT[:, p0:p0 + SEG], ta, tb, ALU.add)

        if full:
            sl = load_slab([(0, 512)], KT, 0, win)
            for mt in range(4):
                def cons(o, mt=mt):
                    ACT(cq[:, mt, :], o, AF.Identity, scale=vec(f"qg{l}", mt))
                    ACT(sqs[mt % 2], o, AF.Square)
                    P.op("pe", (lambda e: e.matmul(pst, lhsT=ones, rhs=sqs[mt % 2], start=(mt == 0), stop=(mt == 3))),
                         [ones, sqs[mt % 2]], [pst])
                proj(sl, mt * 128, KT, xa, cons)
            ACT(ta, pst, AF.Sqrt, bias=RMS_EPS, scale=1.0 / 512)
            RECIP(rq, ta)
            TS("dve", rq, rq, MLA_SCALE, None, ALU.mult)
            TT("dve", C2q, C2, rq, ALU.mult)
            TT("dve", S2q, S2, rq, ALU.mult)

            wq = sview(24576, [4, 384], BF16)
            wxy = sview(27648, [4, 256], BF16)
            wkv = [sview(29696, [2, 256], BF16), sview(30720, [2, 256], BF16)]
            qn = [sview(31744, [SEG], BF16), sview(32768, [SEG], BF16)]
            qr = sview(33792, [SEG], BF16)
            knT = sview(34816, [KEYS], BF16)
            for pr in range(HM // 2):
                h0 = 2 * pr
                DMA("pool", wq, w_uq[l][:, h0 * 192:(h0 + 2) * 192].rearrange("(kt p) m -> p kt m", p=128), "wq")
                w3 = wq.rearrange("p k (h c) -> p k h c", c=192)
                x3 = wxy[:, :, 0:128].rearrange("p k (h c) -> p k h c", c=64)
                y3 = wxy[:, :, 128:256].rearrange("p k (h c) -> p k h c", c=64)
                CP("dve", x3, w3[:, :, :, 128:192])
                CP("dve", y3[:, :, :, 0:32], w3[:, :, :, 160:192])
                CP("dve", y3[:, :, :, 32:64], w3[:, :, :, 128:160])
                bx, by = bank(), bank()
                MM(ps[:, bx, 0:SEG], [(wxy[:, kt, 0:128], cq[:, kt, :]) for kt in range(4)])
                MM(ps[:, by, 0:SEG], [(wxy[:, kt, 128:256], cq[:, kt, :]) for kt in range(4)])
                TT("dve", ta, ps[:, bx, 0:SEG], C2q, ALU.mult)
                TT("dve", tb, ps[:, by, 0:SEG], S2q, ALU.mult)
                TT("dve", qr, ta, tb, ALU.add)
                for hh in range(2):
                    h = h0 + hh
                    bq = bank()
                    MM(ps[:, bq, 0:SEG], [(wq[:, kt, hh * 192:hh * 192 + 128], cq[:, kt, :]) for kt in range(4)])
                    TT("dve", qn[hh], ps[:, bq, 0:SEG], rq, ALU.mult)
                    mla_head(l, seg, h, hh, qn[hh], qr, wkv[hh], knT)

        ret_A(l, seg, 0, full)
        for h in range(HR):
            if h + 1 < HR:
                ret_A(l, seg, h + 1, full)
            ret_B(l, seg, h, full)

    Vt = P.sb("Vt", [128, NKT, 128], BF16)
    PTs = [P.sb(f"PT{i}", [128, SEG], BF16) for i in range(3)]
    pt_i = [0]

    def mla_head(l, seg, h, hh, qn, qr, wkv, knT):
        P.label = "mla_kv"
        p0 = seg * SEG
        nkt = (seg + 1) * NT
        nk = nkt * 128
        DMA("pool", wkv, w_ukv[l][:, h * 256:(h + 1) * 256].rearrange("(kt p) m -> p kt m", p=128), f"wkv{hh}")
        for g in range((nk + 511) // 512):
            n = min(512, nk - g * 512)
            b = bank()
            MM(ps[:, b, 0:n], [(wkv[:, ct, 0:128], ckvT[:, ct, g * 512:g * 512 + n]) for ct in range(2)])
            CP("act", knT[:, g * 512:g * 512 + n], ps[:, b, 0:n])
        for g in range(nkt // 4):
            b = bank()
            MMS([(ps[:, b, j * 128:(j + 1) * 128],
                  [(ckvT[:, ct, (4 * g + j) * 128:(4 * g + j + 1) * 128], wkv[:, ct, 128:256]) for ct in range(2)])
                 for j in range(4)])
            dst = Vt[:, 4 * g:4 * g + 4, :]
            src = ps[:, b, :].rearrange("p (j n) -> p j n", j=4)
            if 4 * g < NPRE * NT and seg >= NPRE:
                TS("dve", dst, src, flag[:, 0:1], None, ALU.mult)
            else:
                CP("dve", dst, src)
        P.label = "mla_att"
        pa, pd = ps[:, 6, 0:SEG], ps[:, 7, 0:SEG]
        base = 64 * hh
        pts = {}

        def score(kt):
            own_t = kt - seg * NT
            q0 = 0 if own_t < 0 else own_t * 128
            b = bank()
            sT = ps[:, b, q0:SEG]
            MM(sT, [(knT[:, kt * 128:(kt + 1) * 128], qn[:, q0:SEG]),
                    (kropeT[base:base + 64, kt * 128:(kt + 1) * 128], qr[base:base + 64, q0:SEG])])
            pt = PTs[pt_i[0] % 3]
            pt_i[0] += 1
            ACT(pt[:, q0:SEG], sT, AF.Exp)
            if own_t >= 0:
                TT("dve", pt[:, q0:q0 + 128], pt[:, q0:q0 + 128], maskb[:], ALU.mult)
            pts[kt] = (pt, q0, own_t)

        def pvacc(kt):
            pt, q0, own_t = pts.pop(kt)
            first = (kt == 0)
            den_l = flagb[:] if (kt < NPRE * NT and seg >= NPRE) else onesb[:]
            if own_t < 0:
                rngs = [(0, SEG, False)]
            else:
                rngs = [(q0, q0 + 128, True)] + ([(q0 + 128, SEG, False)] if q0 + 128 < SEG else [])

            def pv(e, kt=kt, pt=pt, rngs=rngs, first=first, den_l=den_l):
                ins = None
                for (a, b_, stp) in rngs:
                    e.matmul(pa[:, a:b_], lhsT=Vt[:, kt, :], rhs=pt[:, a:b_], start=first, stop=stp)
                    ins = e.matmul(pd[:, a:b_], lhsT=den_l, rhs=pt[:, a:b_], start=first, stop=stp)
                return ins
            P.op("pe", pv, [Vt[:, kt, :], pt[:, q0:SEG], den_l], [pa[:, q0:SEG], pd[:, q0:SEG]])

        for kt in range(min(2, nkt)):
            score(kt)
        for kt in range(nkt):
            pvacc(kt)
            if kt + 2 < nkt:
                score(kt + 2)
        rden = sview(4096, [SEG], F32)
        RECIP(rden, pd)
        TT("dve", mixT[:, h, :], pa, rden, ALU.mult)

    def ret_bufs(h):
        base = 8192 + (h % 2) * 10240
        return dict(qT=sview(base, [2, SEG], BF16), kT=sview(base + 2048, [2, SEG], BF16),
                    qd=sview(base + 4096, [2, SEG], BF16), vT=sview(base + 6144, [2, SEG], BF16),
                    sg=sview(base + 8192, [2, SEG], BF16))

    def ret_A(l, seg, h, full):
        P.label = "ret_proj"
        win = w_in[l]
        xa = lambda kt: xT[:, kt, :]
        cosr, sinr = tabs[:, 0, :], tabs[:, 1, :]
        t1 = sview(0, [SEG], F32)
        t2 = sview(2048, [SEG], F32)
        t3 = sview(4096, [SEG], F32)
        t4 = sview(6144, [SEG], F32)
        B = ret_bufs(h)
        qT, kT, qd, vT, sg = B["qT"], B["kT"], B["qd"], B["vT"], B["sg"]
        o_rq, o_rk, o_rv, o_rg = 832, 1856, 2880, 3904
        c_h = h * 256

        def rope(dst, b0, b1, scale):
            p1, p2 = ps[:, b0, 0:SEG], ps[:, b1, 0:SEG]
            TT("dve", t1, p1, cosr, ALU.mult)
            TT("dve", t2, p2, sinr, ALU.mult)
            TT("dve", t3, p2, cosr, ALU.mult)
            TT("dve", t4, p1, sinr, ALU.mult)
            TT("dve", t1, t1, t2, ALU.subtract)
            TT("dve", t3, t3, t4, ALU.add)
            ACT(dst[:, 0, :], t1, AF.Copy, scale=scale)
            ACT(dst[:, 1, :], t3, AF.Copy, scale=scale)

        if full:
            sl = load_slab([(o_rq + c_h, 256), (o_rk + c_h, 256)], KT, 0, win)
        else:
            sl = load_slab([(o_rk + c_h, 256), (o_rv + c_h, 256)], KT, 0, win)
        kc = 256 if full else 0
        if full:
            b0, b1 = bank(), bank()
            MM(ps[:, b0, 0:SEG], [(sl[:, kt, 0:128], xa(kt)) for kt in range(KT)])
            MM(ps[:, b1, 0:SEG], [(sl[:, kt, 128:256], xa(kt)) for kt in range(KT)])
            rope(qT, b0, b1, RET_SCALE)
            for dt_ in range(2):
                for t in range(NT):
                    TT("dve", qd[:, dt_, t * 128:(t + 1) * 128], qT[:, dt_, t * 128:(t + 1) * 128],
                       cst("qdec", 128 * h, 128 * (h + 1)), ALU.mult)
        b0, b1 = bank(), bank()
        MM(ps[:, b0, 0:SEG], [(sl[:, kt, kc:kc + 128], xa(kt)) for kt in range(KT)])
        MM(ps[:, b1, 0:SEG], [(sl[:, kt, kc + 128:kc + 256], xa(kt)) for kt in range(KT)])
        rope(kT, b0, b1, 1.0)
        if full:
            sl = load_slab([(o_rv + c_h, 256), (o_rg + c_h, 256)], KT, 0, win)
            vc = 0
        else:
            vc = 256
        for dt_ in range(2):
            b = bank()
            MM(ps[:, b, 0:SEG], [(sl[:, kt, vc + dt_ * 128:vc + (dt_ + 1) * 128], xa(kt)) for kt in range(KT)])
            CP("act", vT[:, dt_, :], ps[:, b, 0:SEG])
        if full:
            for dt_ in range(2):
                b = bank()
                MM(ps[:, b, 0:SEG], [(sl[:, kt, 256 + dt_ * 128:256 + (dt_ + 1) * 128], xa(kt)) for kt in range(KT)])
                ACT(sg[:, dt_, :], ps[:, b, 0:SEG], AF.Silu)

    def ret_B(l, seg, h, full):
        P.label = "ret_tr"
        lg = math.log1p(-2.0 ** (-5.0 - h))
        cdec = math.exp(lg * 128.0)
        B = ret_bufs(h)
        qT, kT, qd, vT, sg = B["qT"], B["kT"], B["qd"], B["vT"], B["sg"]
        kd = sview(28672, [NT, 256], BF16)
        vk = sview(30720, [NT, 256], BF16)
        of = sview(32768, [2, SEG], F32)
        sbf = [sview(36864 + 1024 * i, [2, 256], BF16) for i in range(4)]
        scT4 = sview(40960, [NT * 128], BF16)
        gm = sview(41984, [SEG], F32)
        gr = sview(44032, [SEG], F32)
        gt = sview(46080, [SEG], F32)
        for t in range(NT):
            bk, bv = bank(), bank()
            pk = ps[:, bk, 0:128].bitcast(BF16)
            pv = ps[:, bv, 0:128].bitcast(BF16)
            TR([(pk[:, dt_ * 128:(dt_ + 1) * 128], kT[:, dt_, t * 128:(t + 1) * 128]) for dt_ in range(2)], identb[:])
            TR([(pv[:, dt_ * 128:(dt_ + 1) * 128], vT[:, dt_, t * 128:(t + 1) * 128]) for dt_ in range(2)], identb[:])
            TS("dve", kd[:, t, :], pk, cst("kdec", h, h + 1), None, ALU.mult)
            CP("act", vk[:, t, :], pv)
        st = state[:, h, :, :]
        if seg == 0:
            MSET("dve", st, 0.0)
        P.label = "ret_scan"
        dth = cst("dt", 128 * h, 128 * (h + 1))
        if full:
            b = bank()
            MMS([(ps[:, b, t * 128:(t + 1) * 128],
                  [(kT[:, dt_, t * 128:(t + 1) * 128], qT[:, dt_, t * 128:(t + 1) * 128]) for dt_ in range(2)])
                 for t in range(NT)])
            for t in range(NT):
                TT("dve", scT4[:, t * 128:(t + 1) * 128], ps[:, b, t * 128:(t + 1) * 128], dth, ALU.mult)
        kvb = []
        for t in range(NT):
            b = bank()
            MMS([(ps[:, b, dt_ * 256:(dt_ + 1) * 256], [(kd[:, t, dt_ * 128:(dt_ + 1) * 128], vk[:, t, :])])
                 for dt_ in range(2)])
            kvb.append(b)
        for t in range(NT):
            if full:
                CP("act", sbf[t], st)
            STT("dve", st, st, cdec, ps[:, kvb[t], :].rearrange("p (a n) -> p a n", a=2), ALU.mult, ALU.add)
        if seg == NPRE - 1:
            TS("dve", st, st, flag[:, 0:1], None, ALU.mult)
        if not full:
            return
        for t2_ in range(0, NT, 2):
            b = bank()
            grp = []
            for t in range(t2_, min(t2_ + 2, NT)):
                ts_ = slice(t * 128, (t + 1) * 128)
                for dv in range(2):
                    o_ = ps[:, b, ((t - t2_) * 2 + dv) * 128:((t - t2_) * 2 + dv + 1) * 128]
                    grp.append((o_, [(vk[:, t, dv * 128:(dv + 1) * 128], scT4[:, ts_])]
                                + [(sbf[t][:, dt_, dv * 128:(dv + 1) * 128], qd[:, dt_, ts_]) for dt_ in range(2)]))
            MMS(grp)
            for t in range(t2_, min(t2_ + 2, NT)):
                src = ps[:, b, (t - t2_) * 256:(t - t2_ + 1) * 256].rearrange("p (a n) -> p a n", a=2)
                CP("act", of[:, :, t * 128:(t + 1) * 128], src)
        P.label = "ret_gn"
        pm, pq = ps[:, 6, 0:SEG], ps[:, 7, 0:SEG]
        MM(pm, [(ones, of[:, dv, :]) for dv in range(2)])
        for dv in range(2):
            ACT(gt, of[:, dv, :], AF.Square)
            P.op("pe", (lambda e, dv=dv: e.matmul(pq, lhsT=ones, rhs=gt, start=(dv == 0), stop=(dv == 1))),
                 [ones, gt], [pq])
        TS("dve", gm, pm, 1.0 / 256, None, ALU.mult)
        TT("dve", gt, gm, gm, ALU.mult)
        STT("dve", gt, pq, 1.0 / 256, gt, ALU.mult, ALU.subtract)
        ACT(gt, gt, AF.Sqrt, bias=GN_EPS, scale=1.0)
        RECIP(gr, gt)
        for dv in range(2):
            TT("dve", of[:, dv, :], of[:, dv, :], gm, ALU.subtract)
            TT("dve", of[:, dv, :], of[:, dv, :], gr, ALU.mult)
            ACT(of[:, dv, :], of[:, dv, :], AF.Identity, bias=vec(f"gnb{l}", 2 * h + dv), scale=vec(f"gng{l}", 2 * h + dv))
            TT("dve", mixT[:, 8 + 2 * h + dv, :], of[:, dv, :], sg[:, dv, :], ALU.mult)

    def out_proj(l):
        P.label = "out_proj"
        for g in range(4):
            sl = load_slab([(g * 512, 512)], KT, 0, w_out[l])
            for j in range(4):
                mt = 4 * g + j
                proj(sl, j * 128, KT, lambda kt: mixT[:, kt, :],
                     lambda o, mt=mt: STT("dve", xres[:, mt, :], xres[:, mt, :], ALPHA, o, ALU.mult, ALU.add))
                if mt >= 2:
                    stats_tile(mt - 2)
        stats_tile(KT - 2)
        stats_tile(KT - 1)

    def ffn(l):
        P.label = "ffn"
        hb = mixT
        parts = [(0, 12), (12, 10), (22, 12), (34, 10)]
        sgt = [sview(0, [SEG], F32), sview(2048, [SEG], F32)]
        for pi, (f0, nf) in enumerate(parts):
            P.label = "ffn_gu"
            for s2 in range(nf // 2):
                c0 = (f0 + 2 * s2) * 128
                sl = load_slab([(c0, 256)], KT, 0, w_gate[l])
                i = (slab_i[0] - 1) % NSLAB
                DMA("pool", sl[:, :, 256:512], w_up[l][:, c0:c0 + 256].rearrange("(kt p) m -> p kt m", p=128), f"w{i}_1")
                for j in range(2):
                    bg, bu = bank(), bank()
                    MM(ps[:, bg, 0:SEG], [(sl[:, kt, j * 128:(j + 1) * 128], xT[:, kt, :]) for kt in range(KT)])
                    MM(ps[:, bu, 0:SEG], [(sl[:, kt, 256 + j * 128:256 + (j + 1) * 128], xT[:, kt, :]) for kt in range(KT)])
                    s_ = sgt[j % 2]
                    ACT(s_, ps[:, bg, 0:SEG], AF.Silu)
                    TT("dve", hb[:, 2 * s2 + j, :], ps[:, bu, 0:SEG], s_, ALU.mult)
            P.label = "ffn_down"
            for g in range(4):
                sl = load_slab([(g * 512, 512)], nf, f0 * 128, w_down[l])
                for j in range(4):
                    mt = 4 * g + j
                    if pi == 0:
                        cons = lambda o, mt=mt: STT("dve", xres[:, mt, :], xres[:, mt, :], ALPHA, o, ALU.mult, ALU.add)
                    else:
                        cons = lambda o, mt=mt: TT("dve", xres[:, mt, :], xres[:, mt, :], o, ALU.add)
                    proj(sl, j * 128, nf, lambda kt: hb[:, kt, :], cons)
                    if pi == len(parts) - 1 and mt >= 2:
                        stats_tile(mt - 2)
        stats_tile(KT - 2)
        stats_tile(KT - 1)

    def full_layer(l, seg, prefetch=None, after_kt=None, rope_next=None):
        mixers(l, seg, True)
        if rope_next is not None:
            rope_tables(rope_next)
        if prefetch is not None:
            prefetch_x(prefetch)
        out_proj(l)
        layer_norm(f"ln1_g{l}", f"ln1_b{l}", fused_stats=True)
        ffn(l)
        layer_norm(f"ln2_g{l}", f"ln2_b{l}", fused_stats=True, after_kt=after_kt, write_xT=(l < last))

    last = n_layers - 1
    PAIRS = [[2 * i, 2 * i + 1] for i in range(n_pairs)]
    def nxt(l, seg):
        if seg + 1 < NSEG:
            return seg + 1
        return 0 if l < last else None

    rope_tables(0)
    for seg in range(NSEG):
        own = seg - NPRE
        load_x(seg)
        if own < 0:
            mixers(0, seg, False)
            rope_tables(seg + 1)
            prefetch_x(seg + 1)
            continue
        if last == 0:
            full_layer(0, seg, prefetch=seg + 1, rope_next=nxt(0, seg))
            store_out(own)
            continue

        def spill(kt, own=own):
            if kt % 4 != 3:
                return
            c = kt // 4
            cols = slice(4 * c * SEG, (4 * c + 4) * SEG)
            DMA("sp", spb[own][:, cols].rearrange("p (k n) -> p k n", k=4), xT[:, 4 * c:4 * c + 4, :], "spl")
            DMA("sp", spf[own][:, cols].rearrange("p (k n) -> p k n", k=4), xres[:, 4 * c:4 * c + 4, :], "spl2")
        full_layer(0, seg, prefetch=seg + 1, after_kt=spill, rope_next=nxt(0, seg))
        P.op("pool", lambda e, own=own: e.collective_compute("AllGather", ALU.bypass, replica_groups=PAIRS,
                                                              ins=[spb[own]], outs=[gath[own]]),
             [], [], dma="cc", inc=1, extra=[("spl", P.cnt["spl"])])
    if last >= 1:
        for seg in range(NSEG):
            own = seg - NPRE
            if seg == 1:
                flush_deferred()
            if own < 0:
                DMA("sp", xT[:], gath[seg][0:128, :].rearrange("p (k n) -> p k n", k=KT), "spl",
                    extra=[("cc", seg + 1)])
                mixers(1, seg, False)
                rope_tables(seg + 1)
            else:
                DMA("sp", xT[:], spb[own].rearrange("p (k n) -> p k n", k=KT), "spl")
                DMA("sp", xres[:], spf[own].rearrange("p (k n) -> p k n", k=KT), "spl2")
                full_layer(1, seg, rope_next=nxt(1, seg))
                store_out(own)

    import os as _os
    if _os.environ.get("KDBG_WAITLAB"):
        import json as _json
        _json.dump(dict(names=P.names, lab=P.waitlab), open(_os.environ["KDBG_WAITLAB"], "w"))
    P.emit(["xo0", "xo1"])
    stack.close()
    return nc


_NC_CACHE = {}


def _get_nc(SEG, NPRE, NOWN, n_layers=DEPTH, n_pairs=4):
    key = (SEG, NPRE, NOWN, n_layers, n_pairs)
    if key not in _NC_CACHE:
        _NC_CACHE[key] = build(SEG, NPRE, NOWN, n_layers, n_pairs)
    return _NC_CACHE[key]


def run_cores(inp, SEG, n_layers=DEPTH, cores=None):
    x = np.asarray(inp["x"], np.float32)
    posn = np.asarray(inp["positions"], np.int32)
    B, S, _ = x.shape
    half = S // 2
    NPRE = NOWN = half // SEG
    nc = _get_nc(SEG, NPRE, NOWN, n_layers, B)
    vecs = pack_vecs(inp)
    cstv = pack_cst()
    wnames = ("w_in", "w_uq", "w_ukv", "w_out", "w_gate", "w_up", "w_down")
    W = {k: np.ascontiguousarray(np.asarray(inp[k], np.float32)) for k in wnames}
    in_maps = []
    ids = []
    for b in range(B):
        for h in range(2):
            if cores is not None and (b, h) not in cores:
                continue
            if h == 1:
                xi, pi = x[b], posn[b]
            else:
                xi = np.concatenate([x[b, :half], x[b, :half]], axis=0)
                pi = np.concatenate([posn[b, :half], posn[b, :half]], axis=0)
            m = dict(xin=np.ascontiguousarray(xi), pos=np.ascontiguousarray(pi),
                     flag=np.full((128, 1), float(h), np.float32), vecs=vecs, cst=cstv)
            m.update(W)
            in_maps.append(m)
            ids.append((b, h))
    res = run_bass_kernel_spmd(nc, in_maps, core_ids=list(range(len(in_maps))))
    out = np.zeros((B, S, D), np.float32)
    for (b, h), r in zip(ids, res.results):
        out[b, h * half:(h + 1) * half] = r["out"]
    return out


def kernel(**inputs):
    return run_cores(inputs, SEG=512)
```
